# Optimizing a Trainium2 kernel written in Bass

```python
import math
import jax, jax.numpy as jnp
from jax import lax
import numpy as np

D_MODEL = 1024
BATCH = 2
SEQ = 16384
DEPTH = 2

PLE_DIM = 256
D_FF = 4 * D_MODEL
NORM_EPS = 1e-6
ROPE_THETA = 500000.0
ROPE_FRACTION = 4
Q_BLOCK = 128

SB_HEADS = 8
SB_HEAD_DIM = 64
DIFF_HEADS = 4
DIFF_HEAD_DIM = 64
DIFF_V_DIM = 2 * DIFF_HEAD_DIM
SB_WIDTH = SB_HEADS * SB_HEAD_DIM
DIFF_QK_WIDTH = DIFF_HEADS * 2 * DIFF_HEAD_DIM
DIFF_V_WIDTH = DIFF_HEADS * DIFF_V_DIM
EVEN_IN_WIDTH = 3 * SB_WIDTH + 2 * DIFF_QK_WIDTH + DIFF_V_WIDTH
EVEN_OUT_WIDTH = SB_WIDTH + DIFF_V_WIDTH

MOBA_HEADS = 16
MOBA_HEAD_DIM = 64
MOBA_WIDTH = MOBA_HEADS * MOBA_HEAD_DIM
MOBA_BLOCK = 256
MOBA_TOPK = 3
MOBA_Q_CHUNK = 64

N_EVEN = (DEPTH + 1) // 2
N_ODD = DEPTH // 2

kernel_name = "hybrid_stickbreak_diff_moba_trunk"


def rms_norm(x, g):
    xf = x.astype(jnp.float32)
    y = xf * lax.rsqrt(jnp.mean(xf * xf, axis=-1, keepdims=True) + NORM_EPS)
    return (y * g.astype(jnp.float32)).astype(x.dtype)


def rope_tables(positions, head_dim):
    rot = head_dim // ROPE_FRACTION
    inv_freq = ROPE_THETA ** (-(jnp.arange(0, rot, 2, dtype=jnp.float32) / rot))
    ang = positions.astype(jnp.float32)[..., None] * inv_freq
    return jnp.cos(ang), jnp.sin(ang)


def apply_partial_rope(x, cos, sin):
    half = cos.shape[-1]
    rot = 2 * half
    c = cos[:, :, None, :]
    s = sin[:, :, None, :]
    x1 = x[..., :half].astype(jnp.float32)
    x2 = x[..., half:rot].astype(jnp.float32)
    r = jnp.concatenate([x1 * c - x2 * s, x1 * s + x2 * c], axis=-1).astype(x.dtype)
    return jnp.concatenate([r, x[..., rot:]], axis=-1)


def to_heads(t, n_heads):
    b, s, _ = t.shape
    return t.reshape(b, s, n_heads, -1).transpose(0, 2, 1, 3)


def from_heads(t):
    b, h, s, d = t.shape
    return t.transpose(0, 2, 1, 3).reshape(b, s, h * d)


def stick_breaking_attention(q, k, v):
    B, H, S, d = q.shape
    nqb = S // Q_BLOCK
    scale = d ** -0.5
    kpos = jnp.arange(S)
    qb = q.reshape(B, H, nqb, Q_BLOCK, d).transpose(2, 0, 1, 3, 4)

    def one_block(args):
        qi, bi = args
        qpos = bi * Q_BLOCK + jnp.arange(Q_BLOCK)
        z = jnp.einsum('bhqd,bhkd->bhqk', qi, k).astype(jnp.float32) * scale
        past = kpos[None, :] < qpos[:, None]
        log_keep = jnp.where(past, jax.nn.log_sigmoid(-z), 0.0)
        tail = lax.cumsum(log_keep, axis=3, reverse=True) - log_keep
        w = jnp.where(past, jnp.exp(jax.nn.log_sigmoid(z) + tail), 0.0)
        return jnp.einsum('bhqk,bhkd->bhqd', w.astype(v.dtype), v)

    out = lax.map(one_block, (qb, jnp.arange(nqb)))
    return out.transpose(1, 2, 0, 3, 4).reshape(B, H, S, d)


def diff_attention(q1, q2, k1, k2, v, lam):
    B, H, S, d = q1.shape
    nqb = S // Q_BLOCK
    scale = d ** -0.5
    kpos = jnp.arange(S)
    q1b = q1.reshape(B, H, nqb, Q_BLOCK, d).transpose(2, 0, 1, 3, 4)
    q2b = q2.reshape(B, H, nqb, Q_BLOCK, d).transpose(2, 0, 1, 3, 4)
    lam_b = lam[None, :, None, None]

    def one_block(args):
        q1i, q2i, bi = args
        qpos = bi * Q_BLOCK + jnp.arange(Q_BLOCK)
        causal = kpos[None, :] <= qpos[:, None]

        def probs(qi, ki):
            s = jnp.einsum('bhqd,bhkd->bhqk', qi, ki).astype(jnp.float32) * scale
            return jax.nn.softmax(jnp.where(causal, s, -jnp.inf), axis=-1)

        a = probs(q1i, k1) - lam_b * probs(q2i, k2)
        return jnp.einsum('bhqk,bhkd->bhqd', a.astype(v.dtype), v)

    out = lax.map(one_block, (q1b, q2b, jnp.arange(nqb)))
    return out.transpose(1, 2, 0, 3, 4).reshape(B, H, S, v.shape[-1])


def moba_attention(q, k, v):
    B, H, S, d = q.shape
    nb = -(-S // MOBA_BLOCK)
    pad = nb * MOBA_BLOCK - S
    kp = jnp.pad(k, ((0, 0), (0, 0), (0, pad), (0, 0)))
    vp = jnp.pad(v, ((0, 0), (0, 0), (0, pad), (0, 0)))
    kblk = kp.reshape(B, H, nb, MOBA_BLOCK, d)
    vblk = vp.reshape(B, H, nb, MOBA_BLOCK, d)
    kmean = kblk.mean(axis=3)
    n_sel = min(MOBA_TOPK, nb)
    nqc = S // MOBA_Q_CHUNK
    scale = d ** -0.5
    qc = q.reshape(B, H, nqc, MOBA_Q_CHUNK, d).transpose(2, 0, 1, 3, 4)
    blk_ids = jnp.arange(nb)
    bidx = jnp.arange(B)[:, None, None, None]
    hidx = jnp.arange(H)[None, :, None, None]

    def one_chunk(args):
        qi, ci = args
        q0 = ci * MOBA_Q_CHUNK
        own = q0 // MOBA_BLOCK
        qpos = q0 + jnp.arange(MOBA_Q_CHUNK)
        gate = jnp.einsum('bhqd,bhnd->bhqn', qi, kmean).astype(jnp.float32)
        gate = jnp.where(blk_ids < own, gate, -jnp.inf)
        _, gidx = lax.top_k(gate, n_sel)
        sel_valid = gidx < own
        ksel = kblk[bidx, hidx, gidx]
        vsel = vblk[bidx, hidx, gidx]
        s_sel = jnp.einsum('bhqd,bhqnkd->bhqnk', qi, ksel).astype(jnp.float32) * scale
        s_sel = jnp.where(sel_valid[..., None], s_sel, -jnp.inf)
        s_sel = s_sel.reshape(B, H, MOBA_Q_CHUNK, n_sel * MOBA_BLOCK)
        k_own = lax.dynamic_slice_in_dim(kp, own * MOBA_BLOCK, MOBA_BLOCK, axis=2)
        v_own = lax.dynamic_slice_in_dim(vp, own * MOBA_BLOCK, MOBA_BLOCK, axis=2)
        own_pos = own * MOBA_BLOCK + jnp.arange(MOBA_BLOCK)
        s_own = jnp.einsum('bhqd,bhkd->bhqk', qi, k_own).astype(jnp.float32) * scale
        s_own = jnp.where(own_pos[None, :] <= qpos[:, None], s_own, -jnp.inf)
        probs = jax.nn.softmax(jnp.concatenate([s_sel, s_own], axis=-1), axis=-1)
        p_sel = probs[..., :n_sel * MOBA_BLOCK].reshape(B, H, MOBA_Q_CHUNK, n_sel, MOBA_BLOCK)
        p_own = probs[..., n_sel * MOBA_BLOCK:]
        return (jnp.einsum('bhqnk,bhqnkd->bhqd', p_sel.astype(v.dtype), vsel)
                + jnp.einsum('bhqk,bhkd->bhqd', p_own.astype(v.dtype), v_own))

    out = lax.map(one_chunk, (qc, jnp.arange(nqc)))
    return out.transpose(1, 2, 0, 3, 4).reshape(B, H, S, d)


def even_mixer(u, w_in, w_out, lq1, lk1, lq2, lk2, subln_g, cos, sin, layer_idx):
    B, S, _ = u.shape
    proj = u @ w_in
    cuts = list(np.cumsum([SB_WIDTH, SB_WIDTH, SB_WIDTH, DIFF_QK_WIDTH, DIFF_QK_WIDTH]))
    a_q, a_k, a_v, b_q, b_k, b_v = jnp.split(proj, [int(c) for c in cuts], axis=-1)
    o_a = stick_breaking_attention(to_heads(a_q, SB_HEADS), to_heads(a_k, SB_HEADS),
                                   to_heads(a_v, SB_HEADS))
    bq = b_q.reshape(B, S, DIFF_HEADS, 2, DIFF_HEAD_DIM)
    bk = b_k.reshape(B, S, DIFF_HEADS, 2, DIFF_HEAD_DIM)
    rq = lambda t: apply_partial_rope(t, cos, sin).transpose(0, 2, 1, 3)
    q1, q2 = rq(bq[..., 0, :]), rq(bq[..., 1, :])
    k1, k2 = rq(bk[..., 0, :]), rq(bk[..., 1, :])
    lam_init = 0.8 - 0.6 * math.exp(-0.3 * layer_idx)
    f32 = jnp.float32
    lam = (jnp.exp(jnp.sum(lq1.astype(f32) * lk1.astype(f32), axis=-1))
           - jnp.exp(jnp.sum(lq2.astype(f32) * lk2.astype(f32), axis=-1)) + lam_init)
    o_b = diff_attention(q1, q2, k1, k2, to_heads(b_v, DIFF_HEADS), lam)
    o_b = rms_norm(o_b, subln_g) * (1.0 - lam_init)
    merged = jnp.concatenate([from_heads(o_a), from_heads(o_b)], axis=-1)
    return merged @ w_out


def odd_mixer(u, w_in, w_out, cos, sin):
    B, S, _ = u.shape
    q, k, v = jnp.split(u @ w_in, 3, axis=-1)
    q = apply_partial_rope(q.reshape(B, S, MOBA_HEADS, MOBA_HEAD_DIM), cos, sin).transpose(0, 2, 1, 3)
    k = apply_partial_rope(k.reshape(B, S, MOBA_HEADS, MOBA_HEAD_DIM), cos, sin).transpose(0, 2, 1, 3)
    o = moba_attention(q, k, to_heads(v, MOBA_HEADS))
    return from_heads(o) @ w_out


def squared_relu_mlp(u, w1, w2):
    return jnp.square(jax.nn.relu(u @ w1)) @ w2


def setup_inputs(seed: int = 0) -> dict:
    key = jax.random.key(seed)
    ks = jax.random.split(key, 24)
    f32 = jnp.float32
    nrm = lambda k, shape, fan_in: jax.random.normal(k, shape, f32) * (fan_in ** -0.5)
    gain = lambda k, shape: 1.0 + 0.01 * jax.random.normal(k, shape, f32)
    x = jax.random.normal(ks[0], (BATCH, SEQ, D_MODEL), f32)
    p = jax.random.normal(ks[1], (DEPTH, BATCH, SEQ, PLE_DIM), f32)
    offsets = jax.random.randint(ks[2], (BATCH, 1), 0, 4096, dtype=jnp.int32)
    positions = (offsets + jnp.arange(SEQ, dtype=jnp.int32)[None, :]).astype(jnp.int32)
    return {
        "x": x,
        "p": p,
        "positions": positions,
        "attn_norm": gain(ks[3], (DEPTH, D_MODEL)),
        "ab_w_in": nrm(ks[4], (N_EVEN, D_MODEL, EVEN_IN_WIDTH), D_MODEL),
        "ab_w_out": nrm(ks[5], (N_EVEN, EVEN_OUT_WIDTH, D_MODEL), EVEN_OUT_WIDTH),
        "diff_lam_q1": 0.1 * jax.random.normal(ks[6], (N_EVEN, DIFF_HEADS, DIFF_HEAD_DIM), f32),
        "diff_lam_k1": 0.1 * jax.random.normal(ks[7], (N_EVEN, DIFF_HEADS, DIFF_HEAD_DIM), f32),
        "diff_lam_q2": 0.1 * jax.random.normal(ks[8], (N_EVEN, DIFF_HEADS, DIFF_HEAD_DIM), f32),
        "diff_lam_k2": 0.1 * jax.random.normal(ks[9], (N_EVEN, DIFF_HEADS, DIFF_HEAD_DIM), f32),
        "diff_subln": gain(ks[10], (N_EVEN, DIFF_V_DIM)),
        "moba_w_in": nrm(ks[11], (N_ODD, D_MODEL, 3 * MOBA_WIDTH), D_MODEL),
        "moba_w_out": nrm(ks[12], (N_ODD, MOBA_WIDTH, D_MODEL), MOBA_WIDTH),
        "mlp_norm": gain(ks[13], (DEPTH, D_MODEL)),
        "w_ff1": nrm(ks[14], (DEPTH, D_MODEL, D_FF), D_MODEL),
        "w_ff2": nrm(ks[15], (DEPTH, D_FF, D_MODEL), D_FF),
        "ple_norm": gain(ks[16], (DEPTH, D_MODEL)),
        "ple_gate": nrm(ks[17], (DEPTH, D_MODEL, D_MODEL), D_MODEL),
        "ple_proj": nrm(ks[18], (DEPTH, PLE_DIM, D_MODEL), PLE_DIM),
        "final_norm": gain(ks[19], (D_MODEL,)),
    }


def reference(x, p, positions, attn_norm, ab_w_in, ab_w_out, diff_lam_q1, diff_lam_k1,
              diff_lam_q2, diff_lam_k2, diff_subln, moba_w_in, moba_w_out, mlp_norm,
              w_ff1, w_ff2, ple_norm, ple_gate, ple_proj, final_norm):
    cos, sin = rope_tables(positions, DIFF_HEAD_DIM)
    h = x
    for i in range(DEPTH):
        u = rms_norm(h, attn_norm[i])
        if i % 2 == 0:
            j = i // 2
            mix = even_mixer(u, ab_w_in[j], ab_w_out[j], diff_lam_q1[j], diff_lam_k1[j],
                             diff_lam_q2[j], diff_lam_k2[j], diff_subln[j], cos, sin, i)
        else:
            j = i // 2
            mix = odd_mixer(u, moba_w_in[j], moba_w_out[j], cos, sin)
        h = h + mix
        h = h + squared_relu_mlp(rms_norm(h, mlp_norm[i]), w_ff1[i], w_ff2[i])
        gate = jax.nn.sigmoid(rms_norm(h, ple_norm[i]) @ ple_gate[i])
        h = h + gate * (p[i].astype(h.dtype) @ ple_proj[i])
    return rms_norm(h, final_norm)
```

```python
import math
import time as _time
import numpy as np
import ml_dtypes
from contextlib import ExitStack
import concourse.bass as bass
import concourse.mybir as mybir
from concourse.bass_utils import run_bass_kernel_spmd

F32 = mybir.dt.float32
BF16 = mybir.dt.bfloat16
I32 = mybir.dt.int32
AF = mybir.ActivationFunctionType
ALU = mybir.AluOpType
AX = mybir.AxisListType

D = 1024
DFF = 4096
EPS = 1e-6
NEG = -30000.0
TWO_PI_HI = 6.28125
TWO_PI_LO = float(2 * np.pi - 6.28125)


class Buf:
    __slots__ = ("name", "w", "r")

    def __init__(self, name=""):
        self.name = name
        self.w = None
        self.r = {}


class DmaSem:
    def __init__(self, fw, name):
        self.h = fw.es.enter_context(fw.nc.semaphore(name))
        self.v = 0
        self.id = name


class Eng:
    def __init__(self, fw, name, handle, has_sem=True):
        self.name = name
        self.h = handle
        self.sem = fw.es.enter_context(fw.nc.semaphore("s_" + name)) if has_sem else None
        self.nsig = 0
        self.sigs = []
        self.waited = {}
        self.pending = []


class FW:
    def __init__(self, nc):
        self.nc = nc
        self.es = ExitStack()
        self.engs = {}
        for n, h in (("pe", nc.tensor), ("act", nc.scalar), ("dve", nc.vector),
                     ("pool", nc.gpsimd), ("sp", nc.sync)):
            self.engs[n] = Eng(self, n, h, has_sem=(n != "sp"))
        self.nsem = 0
        self.dsems = []

    def dsem(self, name=None):
        self.nsem += 1
        d = DmaSem(self, name or f"d{self.nsem}")
        self.dsems.append(d)
        return d

    def sbuf(self, name, shape, dt):
        return self.es.enter_context(self.nc.sbuf_tensor(name, list(shape), dt))

    def psum(self, name, shape, dt):
        return self.es.enter_context(self.nc.psum_tensor(name, list(shape), dt))

    def _need(self, e, tok):
        if tok is None:
            return
        if tok[0] == "E":
            _, e2, idx = tok
            if e2 is e and e.name == "pe":
                return
            v = None
            for j in range(idx, len(e2.sigs)):
                if e2.sigs[j] is not None:
                    v = e2.sigs[j]
                    break
            assert v is not None, f"unsignaled producer {e2.name}#{idx} needed by {e.name}"
            key = "E" + e2.name
            if e.waited.get(key, 0) >= v:
                return
            e.waited[key] = v
            sem = e2.sem
            e.pending.append(lambda h, sem=sem, v=v: h.wait_ge(sem, v))
        else:
            _, ds, v = tok
            key = "D" + ds.id
            if e.waited.get(key, 0) >= v:
                return
            e.waited[key] = v
            e.pending.append(lambda h, ds=ds, v=v: h.wait_ge(ds.h, v))

    def _deps(self, e, reads, writes):
        for b in reads:
            self._need(e, b.w)
        for b in writes:
            self._need(e, b.w)
            for t in b.r.values():
                self._need(e, t)

    def _mark(self, tok, key, reads, writes):
        for b in reads:
            b.r[key] = tok
        for b in writes:
            b.w = tok
            b.r = {}

    def op(self, en, fn, reads=(), writes=(), sig=True):
        e = self.engs[en]
        self._deps(e, reads, writes)
        idx = len(e.sigs)
        if sig:
            e.nsig += 1
            sem = e.sem
            e.pending.append(lambda h, fn=fn, sem=sem: fn(h).then_inc(sem, 1))
            e.sigs.append(e.nsig)
        else:
            e.pending.append(lambda h, fn=fn: fn(h))
            e.sigs.append(None)
        tok = ("E", e, idx)
        self._mark(tok, "E" + e.name, reads, writes)
        return tok

    def dma(self, en, out, in_, ds, reads=(), writes=()):
        e = self.engs[en]
        self._deps(e, reads, writes)
        ds.v += 16
        e.pending.append(lambda h, out=out, in_=in_, ds=ds: h.dma_start(out=out, in_=in_).then_inc(ds.h, 16))
        e.sigs.append(None)
        tok = ("D", ds, ds.v)
        self._mark(tok, "D" + ds.id, reads, writes)
        return tok

    def barrier(self, scratch):
        toks = []
        pool = self.engs["pool"]
        for ds in self.dsems:
            if ds.v:
                self._need(pool, ("D", ds, ds.v))
        bufs = [Buf() for _ in range(4)]
        toks.append(self.op("pool", lambda h: h.memset(scratch[:, 0:1], 0.0), writes=[bufs[0]]))
        toks.append(self.op("dve", lambda h: h.memset(scratch[:, 1:2], 0.0), writes=[bufs[1]]))
        toks.append(self.op("act", lambda h: h.activation(out=scratch[:, 2:3], in_=scratch[:, 3:4], func=AF.Copy), writes=[bufs[2]]))
        pe = self.engs["pe"]
        if pe.sigs and pe.sigs[-1] is None:
            raise AssertionError("last PE op before barrier must be signaled")
        petok = ("E", pe, len(pe.sigs) - 1) if pe.sigs else None
        for n in ("pe", "act", "dve", "pool", "sp"):
            e = self.engs[n]
            for t in toks:
                self._need(e, t)
            if petok is not None:
                self._need(e, petok)
            for ds in self.dsems:
                if ds.v:
                    self._need(e, ("D", ds, ds.v))

    def finish(self, final_toks, en="sp"):
        for t in final_toks:
            self._need(self.engs[en], t)
        with self.nc.Block() as block:
            for n, dec in (("pe", block.tensor), ("act", block.scalar), ("dve", block.vector),
                           ("pool", block.gpsimd), ("sp", block.sync)):
                e = self.engs[n]
                if not e.pending:
                    continue

                def body(h, e=e):
                    for f in e.pending:
                        f(h)
                dec(body)
        self.es.close()


class Arena:
    def __init__(self, fw, nelem):
        self.t = fw.sbuf("arena", [128, nelem], BF16)
        self.n = nelem
        self.off = 0

    def reset(self, to=0):
        self.off = to

    def bf(self, n):
        assert self.off + n <= self.n, f"arena overflow {self.off}+{n}>{self.n}"
        a = self.t[:, self.off:self.off + n]
        self.off += n
        return a

    def f32(self, n):
        if self.off % 2:
            self.off += 1
        return self.bf(2 * n).bitcast(F32)

    def i32(self, n):
        if self.off % 2:
            self.off += 1
        return self.bf(2 * n).bitcast(I32)


def ACT(out, in_, func, **kw):
    return lambda h: h.activation(out=out, in_=in_, func=func, **kw)


def TTo(out, in0, in1, op):
    return lambda h: h.tensor_tensor(out=out, in0=in0, in1=in1, op=op)


def TS(out, in0, s1, s2, op0, op1=None):
    if op1 is None:
        return lambda h: h.tensor_scalar(out=out, in0=in0, scalar1=s1, scalar2=None, op0=op0)
    return lambda h: h.tensor_scalar(out=out, in0=in0, scalar1=s1, scalar2=s2, op0=op0, op1=op1)


def STT(out, in0, scalar, in1, op0, op1):
    return lambda h: h.scalar_tensor_tensor(out=out, in0=in0, scalar=scalar, in1=in1, op0=op0, op1=op1)


def CP(out, in_):
    return lambda h: h.tensor_copy(out=out, in_=in_)


def MS(ap, c):
    return lambda h: h.memset(ap, c)


def MM(out, lhsT, rhs, start, stop, skip=False):
    if skip:
        return lambda h: h.matmul(out, lhsT=lhsT, rhs=rhs, start=start, stop=stop, skip_group_check=True)
    return lambda h: h.matmul(out, lhsT=lhsT, rhs=rhs, start=start, stop=stop)


C_ID, C_TS, C_TI, C_NTI, C_NO, C_ON, C_ZE = 0, 128, 256, 384, 512, 640, 768
NCB = 896


def make_consts():
    cb = np.zeros((128, NCB), np.float32)
    i = np.arange(128)
    cb[:, C_ID:C_ID + 128] = np.eye(128)
    cb[:, C_TS:C_TS + 128] = (i[:, None] < i[None, :])
    cb[:, C_TI:C_TI + 128] = (i[:, None] <= i[None, :])
    cb[:, C_NTI:C_NTI + 128] = -1.0 * (i[:, None] >= i[None, :])
    cb[:, C_NO:C_NO + 128] = -1.0
    cb[:, C_ON:C_ON + 128] = 1.0
    cf = np.zeros((128, 8), np.float32)
    inv = (np.float32(500000.0) ** (-(np.arange(0, 16, 2, dtype=np.float32) / np.float32(16)))).astype(np.float32)
    for p in range(128):
        d = p % 64
        if d < 16:
            cf[p, 0] = inv[d % 8]
            cf[p, 1] = -1.0 if d < 8 else 1.0
    return cb, cf


class Ctx:
    pass


def setup_common(nc, fw, S):
    cx = Ctx()
    cx.nc, cx.fw, cx.S = nc, fw, S
    cx.ar = Arena(fw, 90 * 1024)
    cx.ps = [fw.psum(f"ps{i}", [128, 512], F32) for i in range(8)]
    cx.pb = [Buf(f"ps{i}") for i in range(8)]
    cx.cb = fw.sbuf("cb", [128, NCB], BF16)
    cx.cf = fw.sbuf("cf", [128, 8], F32)
    cx.scr = fw.sbuf("scr", [128, 8], F32)
    cx.onesf = fw.sbuf("onesf", [128, 64], F32)
    cx.Bc = Buf("consts")
    d_cb = nc.dram_tensor("cbd", [128, NCB], F32, kind="ExternalInput").ap()
    d_cf = nc.dram_tensor("cfd", [128, 8], F32, kind="ExternalInput").ap()
    ds = fw.dsem("dconst")
    fw.dma("pool", cx.cb[:], d_cb[:, :], ds, writes=[cx.Bc])
    fw.dma("sp", cx.cf[:], d_cf[:, :], ds, writes=[cx.Bc])
    fw.op("dve", MS(cx.onesf[:], 1.0), writes=[cx.Bc])
    fw.op("dve", MS(cx.scr[:], 0.0), writes=[cx.Bc])
    return cx


def cbs(cx, off, rows=slice(0, 128), n=128):
    return cx.cb[rows, off:off + n]


def rms_feature_major(cx, xT, Bx, gcol, sq, Bsq, rb, Brb, uT, BuT, T, bank, out_f32=None):
    fw = cx.fw
    for c in range(8):
        fw.op("act", ACT(sq[:, c * T:(c + 1) * T], xT[:, c * T:(c + 1) * T], AF.Square), reads=[Bx], writes=[Bsq])
    for c in range(8):
        fw.op("pe", MM(cx.ps[bank][:, 0:T], cbs(cx, C_ON), sq[:, c * T:(c + 1) * T], c == 0, c == 7),
              reads=[Bsq, cx.Bc], writes=[cx.pb[bank]], sig=(c == 7))
    fw.op("act", ACT(rb[:, 0:T], cx.ps[bank][:, 0:T], AF.Ln, scale=1.0 / D, bias=cx.epsb[:, 0:1]), reads=[cx.pb[bank], cx.Bc], writes=[Brb])
    fw.op("act", ACT(rb[:, 0:T], rb[:, 0:T], AF.Exp, scale=-0.5), reads=[Brb], writes=[Brb])
    for c in range(8):
        o = uT[:, c * T:(c + 1) * T] if out_f32 is None else out_f32[:, c * T:(c + 1) * T]
        fw.op("dve", STT(o, xT[:, c * T:(c + 1) * T], gcol[:, c:c + 1], rb[:, 0:T], ALU.mult, ALU.mult),
              reads=[Bx, Brb, cx.Bc], writes=[BuT])


def att_phase(cx, layer, dr):
    fw, nc, ar, S = cx.fw, cx.nc, cx.ar, cx.S
    NT = S // 512
    NKB = S // 128
    roped = [False, False, True, True] if layer == 0 else [True, True, True, True]
    ar.reset()
    wc = ar.bf(8 * 768)
    wp = ar.bf(8 * 512)
    Bw = Buf("w")
    dsw = fw.dsem()
    wc3 = wc.rearrange("p (c n) -> p c n", c=8)
    wp3 = wp.rearrange("p (c n) -> p c n", c=8)
    for c in range(8):
        fw.dma("pool", wc3[:, c, :], dr["wc"][c * 128:(c + 1) * 128, :], dsw, writes=[Bw])
        fw.dma("pool", wp3[:, c, :], dr["wp"][c * 128:(c + 1) * 128, :], dsw, writes=[Bw])
    for gi in (0, 2):
        fw.op("dve", TS(wc3[:, :, gi * 128:(gi + 1) * 128], wc3[:, :, gi * 128:(gi + 1) * 128], 0.125, None, ALU.mult), reads=[Bw], writes=[Bw])
        fw.op("dve", TS(wp3[:, :, gi * 128:(gi + 1) * 128], wp3[:, :, gi * 128:(gi + 1) * 128], 0.125, None, ALU.mult), reads=[Bw], writes=[Bw])
    gcol = ar.f32(8)
    fw.dma("sp", gcol, dr["gain"][:, :], dsw, writes=[Bw])
    mark_proj = ar.off
    xin = [ar.f32(8 * 512) for _ in range(2)]
    Bxin = [Buf() for _ in range(2)]
    dsx = [fw.dsem() for _ in range(2)]
    sq = ar.bf(8 * 512); Bsq = Buf()
    rb = ar.f32(512); Brb = Buf()
    uT = [ar.bf(8 * 512) for _ in range(2)]
    BuT = [Buf() for _ in range(2)]
    posi = ar.i32(512); Bposi = Buf(); dspos = fw.dsem()
    ang = ar.f32(512); Bang = Buf()
    ki = ar.i32(512); Bki = Buf()
    kf = ar.f32(512); Bkf = Buf()
    Ct = ar.f32(512); BCt = Buf()
    Snt = ar.f32(512); BSn = Buf()
    t1 = [ar.f32(512) for _ in range(2)]; Bt1 = [Buf() for _ in range(2)]
    t2 = [ar.f32(512) for _ in range(2)]; Bt2 = [Buf() for _ in range(2)]
    qko = [[ar.bf(512) for _ in range(2)] for _ in range(4)]
    Bqko = [[Buf() for _ in range(2)] for _ in range(4)]
    dsq = [[fw.dsem() for _ in range(2)] for _ in range(4)]
    vo = [ar.bf(4 * 256) for _ in range(2)]
    Bvo = [Buf() for _ in range(2)]
    dsv = [fw.dsem() for _ in range(2)]
    qk, vs = dr["qk"], dr["vs"]
    vs3 = vs.rearrange("(n p) c -> p n c", p=128)
    pb, ps = cx.pb, cx.ps
    for st in range(NT):
        k2 = st % 2
        x3 = xin[k2].rearrange("p (c t) -> p c t", c=8)
        for c in range(8):
            fw.dma("sp", x3[:, c, :], dr["xsrc"](c, st), dsx[k2], writes=[Bxin[k2]])
        rms_feature_major(cx, xin[k2], Bxin[k2], gcol, sq, Bsq, rb, Brb, uT[k2], BuT[k2], 512, 0)
        u3 = uT[k2].rearrange("p (c t) -> p c t", c=8)
        fw.dma("sp", posi, dr["pos"][0:1, st * 512:(st + 1) * 512].broadcast_to([128, 512]), dspos, writes=[Bposi])
        fw.op("dve", TS(ang, posi, cx.cf[:, 0:1], None, ALU.mult), reads=[Bposi, cx.Bc], writes=[Bang])
        fw.op("dve", TS(ki, ang, float(1.0 / (2 * np.pi)), None, ALU.mult), reads=[Bang], writes=[Bki])
        fw.op("dve", CP(kf, ki), reads=[Bki], writes=[Bkf])
        fw.op("dve", STT(ang, kf, -TWO_PI_HI, ang, ALU.mult, ALU.add), reads=[Bkf, Bang], writes=[Bang])
        fw.op("dve", STT(ang, kf, -TWO_PI_LO, ang, ALU.mult, ALU.add), reads=[Bkf, Bang], writes=[Bang])
        fw.op("dve", TS(ang, ang, float(np.pi), float(-np.pi), ALU.min, ALU.max), reads=[Bang], writes=[Bang])
        fw.op("act", ACT(kf, ang, AF.Sin), reads=[Bang], writes=[Bkf])
        fw.op("dve", TS(Snt, kf, cx.cf[:, 1:2], None, ALU.mult), reads=[Bkf, cx.Bc], writes=[BSn])
        fw.op("dve", STT(ang, ang, -1.0, ang, ALU.mult, ALU.max), reads=[Bang], writes=[Bang])
        fw.op("act", ACT(Ct, ang, AF.Sin, scale=-1.0, bias=cx.hpib[:, 0:1]), reads=[Bang, cx.Bc], writes=[BCt])
        for gi in range(4):
            b1 = 1 + (gi % 2) * 2
            b2 = b1 + 1
            for c in range(8):
                fw.op("pe", MM(ps[b1][:, :], wc3[:, c, gi * 128:(gi + 1) * 128], u3[:, c, :], c == 0, c == 7),
                      reads=[Bw, BuT[k2]], writes=[pb[b1]], sig=(c == 7))
            if roped[gi]:
                for c in range(8):
                    fw.op("pe", MM(ps[b2][:, :], wp3[:, c, gi * 128:(gi + 1) * 128], u3[:, c, :], c == 0, c == 7),
                          reads=[Bw, BuT[k2]], writes=[pb[b2]], sig=(c == 7))
                j2 = gi % 2
                fw.op("dve", TTo(t1[j2], ps[b1][:, :], Ct, ALU.mult), reads=[pb[b1], BCt], writes=[Bt1[j2]])
                fw.op("dve", TTo(t2[j2], ps[b2][:, :], Snt, ALU.mult), reads=[pb[b2], BSn], writes=[Bt2[j2]])
                fw.op("pool", TTo(qko[gi][k2], t1[j2], t2[j2], ALU.add), reads=[Bt1[j2], Bt2[j2]], writes=[Bqko[gi][k2]])
            else:
                fw.op("act", ACT(qko[gi][k2], ps[b1][:, :], AF.Copy), reads=[pb[b1]], writes=[Bqko[gi][k2]])
            fw.dma("sp", qk[gi][:, st * 512:(st + 1) * 512], qko[gi][k2], dsq[gi][k2], reads=[Bqko[gi][k2]])
        vo3 = vo[k2].rearrange("p (t c) -> p t c", t=4)
        for t in range(4):
            bk = 5 + t // 2
            o = ps[bk][:, (t % 2) * 256:(t % 2) * 256 + 256]
            for c in range(8):
                fw.op("pe", MM(o, u3[:, c, t * 128:(t + 1) * 128], wc3[:, c, 512:768], c == 0, c == 7),
                      reads=[Bw, BuT[k2]], writes=[pb[bk]], sig=(c == 7))
            fw.op("act", ACT(vo3[:, t, :], o, AF.Copy), reads=[pb[bk]], writes=[Bvo[k2]])
        fw.dma("sp", vs3[:, st * 4:(st + 1) * 4, :], vo3, dsv[k2], reads=[Bvo[k2]])
    fw.barrier(cx.scr)
    ar.reset(mark_proj)
    KT = ar.bf(S); BKT = Buf("KT")
    VV = ar.bf(NKB * 130); BVV = Buf("VV")
    dsK = fw.dsem(); dsV = fw.dsem()
    QT = [ar.bf(512) for _ in range(2)]; BQT = [Buf() for _ in range(2)]
    dsQ = [fw.dsem() for _ in range(2)]
    e32 = [ar.f32(512) for _ in range(2)]; Be = [Buf() for _ in range(2)]
    sp16 = [ar.bf(512) for _ in range(2)]; Bsp = [Buf() for _ in range(2)]
    w16 = [ar.bf(512) for _ in range(3)]; Bw16 = [Buf() for _ in range(3)]
    L32 = ar.f32(512); BL32 = Buf()
    L16 = [ar.bf(512) for _ in range(3)]; BL16 = [Buf() for _ in range(3)]
    o16 = [ar.bf(512) for _ in range(2)]; Bo16 = [Buf() for _ in range(2)]
    dsO = [fw.dsem() for _ in range(2)]
    ot = dr["ot"]
    tri_s = cbs(cx, C_TS); tri_i = cbs(cx, C_TI)
    cnt = {"q": 0, "o": 0, "w": 0}

    def load_q(gi, rows_src, rows_dst, qi):
        k = cnt["q"] % 2
        cnt["q"] += 1
        fw.dma("sp", QT[k][rows_dst, :], qk[gi][rows_src, qi * 512:(qi + 1) * 512], dsQ[k], writes=[BQT[k]])
        return k

    def store_o(src_ps, bank, nrows, row0, qi, mul=None):
        k = cnt["o"] % 2
        cnt["o"] += 1
        fw.op("act", ACT(o16[k][0:nrows, :], src_ps, AF.Copy), reads=[pb[bank]], writes=[Bo16[k]])
        fw.dma("sp", ot[row0:row0 + nrows, qi * 512:(qi + 1) * 512], o16[k][0:nrows, :], dsO[k], reads=[Bo16[k]])

    if layer == 0:
        V3 = VV[:, 0:NKB * 128].rearrange("p (n c) -> p n c", c=128)
        fw.dma("sp", KT, qk[1][:, :], dsK, writes=[BKT])
        for n0 in range(0, NKB, 32):
            n1 = min(NKB, n0 + 32)
            fw.dma("sp", V3[:, n0:n1, :], vs3[:, n0:n1, 0:128], dsV, writes=[BVV])
        for qi in range(NT):
            kq = load_q(0, slice(0, 128), slice(0, 128), qi)
            for h in range(2):
                rows = slice(64 * h, 64 * h + 64)
                OB = 5 + h
                fw.op("pe", MM(ps[OB][0:64, :], cbs(cx, C_ZE, n=64), KT[:, 0:512], True, False), reads=[cx.Bc, BKT], writes=[pb[OB]], sig=False)
                fw.op("pool", MS(L32, 0.0), writes=[BL32])
                blocks = list(range(4 * qi + 3, -1, -1))
                nb = len(blocks)

                def stage1(n):
                    kb = blocks[n]
                    j = kb - 4 * qi
                    c0 = 128 * j if j >= 0 else 0
                    a = 1 + n % 2
                    k = n % 2
                    fw.op("pe", MM(ps[a][:, c0:512], KT[rows, kb * 128:(kb + 1) * 128], QT[kq][rows, c0:512], True, True),
                          reads=[BKT, BQT[kq]], writes=[pb[a]])
                    fw.op("act", ACT(e32[k][:, c0:512], ps[a][:, c0:512], AF.Exp), reads=[pb[a]], writes=[Be[k]])
                    fw.op("act", ACT(sp16[k][:, c0:512], e32[k][:, c0:512], AF.Ln, bias=cx.oneb[:, 0:1]), reads=[Be[k], cx.Bc], writes=[Bsp[k]])
                    if j >= 0:
                        fw.op("dve", TTo(sp16[k][:, c0:c0 + 128], sp16[k][:, c0:c0 + 128], tri_s, ALU.mult), reads=[Bsp[k], cx.Bc], writes=[Bsp[k]])
                    if n + 1 < nb:
                        fw.op("dve", TTo(L32[:, c0:512], L32[:, c0:512], sp16[k][:, c0:512], ALU.add), reads=[BL32, Bsp[k]], writes=[BL32])
                        fw.op("dve", CP(L16[(n + 1) % 3], L32), reads=[BL32], writes=[BL16[(n + 1) % 3]])

                def stage2(n):
                    kb = blocks[n]
                    j = kb - 4 * qi
                    c0 = 128 * j if j >= 0 else 0
                    b = 3 + n % 2
                    k = n % 2
                    kw = n % 3
                    fw.op("pe", MM(ps[b][:, c0:512], cbs(cx, C_NTI), sp16[k][:, c0:512], True, False), reads=[cx.Bc, Bsp[k]], writes=[pb[b]], sig=False)
                    if n > 0:
                        fw.op("pe", MM(ps[b][:, c0:512], cbs(cx, C_NO), L16[n % 3][:, c0:512], False, False), reads=[cx.Bc, BL16[n % 3]], writes=[pb[b]], sig=False)
                    fw.op("pe", MM(ps[b][:, c0:512], KT[rows, kb * 128:(kb + 1) * 128], QT[kq][rows, c0:512], False, True),
                          reads=[BKT, BQT[kq]], writes=[pb[b]])
                    fw.op("act", ACT(w16[kw][:, c0:512], ps[b][:, c0:512], AF.Exp), reads=[pb[b]], writes=[Bw16[kw]])
                    if j >= 0:
                        fw.op("dve", TTo(w16[kw][:, c0:c0 + 128], w16[kw][:, c0:c0 + 128], tri_s, ALU.mult), reads=[Bw16[kw], cx.Bc], writes=[Bw16[kw]])
                    fw.op("pe", MM(ps[OB][0:64, c0:512], V3[:, kb, 64 * h:64 * h + 64], w16[kw][:, c0:512], False, n == nb - 1, skip=True),
                          reads=[BVV, Bw16[kw]], writes=[pb[OB]], sig=(n == nb - 1))

                for n in range(nb):
                    stage1(n)
                    if n > 0:
                        stage2(n - 1)
                stage2(nb - 1)
                store_o(ps[OB][0:64, :], OB, 64, 64 * h, qi)
        lamt = ar.f32(4 * 64); Blam = Buf()
        lj = ar.f32(64); Blj = Buf()
        sv = ar.f32(8); Bsv = Buf()
        dsl = fw.dsem()
        fw.dma("sp", lamt.rearrange("p (a d) -> p a d", a=4), dr["lamv"].rearrange("(o a) d -> o a d", o=1).broadcast_to([128, 4, 64]), dsl, writes=[Blam])
        fw.dma("sp", sv[:, 4:5], dr["subln"][:, :], dsl, writes=[Bsv])
        fw.op("dve", MS(sv[:, 0:4], 0.0), writes=[Bsv])
        lam_init = 0.8 - 0.6 * math.exp(-0.3 * layer)
        for a in range(2):
            fw.op("dve", TTo(lj, lamt[:, (2 * a) * 64:(2 * a + 1) * 64], lamt[:, (2 * a + 1) * 64:(2 * a + 2) * 64], ALU.mult), reads=[Blam, Bsv], writes=[Blj])
            fw.op("dve", lambda h, a=a: h.tensor_reduce(out=sv[:, a:a + 1], in_=lj, axis=AX.X, op=ALU.add), reads=[Blj], writes=[Bsv])
        fw.op("act", ACT(sv[:, 0:2], sv[:, 0:2], AF.Exp), reads=[Bsv], writes=[Bsv])
        fw.op("dve", TTo(sv[:, 3:4], sv[:, 1:2], sv[:, 0:1], ALU.subtract), reads=[Bsv], writes=[Bsv])
        fw.op("dve", TS(sv[:, 2:3], sv[:, 3:4], -lam_init, None, ALU.add), reads=[Bsv], writes=[Bsv])
        fw.op("dve", TS(sv[:, 5:6], sv[:, 4:5], 1.0 - lam_init, None, ALU.mult), reads=[Bsv], writes=[Bsv])
        fw.dma("sp", KT, qk[3][:, :], dsK, writes=[BKT])
        for n0 in range(0, NKB, 32):
            n1 = min(NKB, n0 + 32)
            fw.dma("sp", V3[:, n0:n1, :], vs3[:, n0:n1, 128:256], dsV, writes=[BVV])
        rd = [ar.f32(512) for _ in range(2)]; Brd = [Buf() for _ in range(2)]
        tt = [ar.f32(512) for _ in range(2)]; Btt = [Buf() for _ in range(2)]
        of = ar.f32(512); Bof = Buf()
        sq16 = ar.bf(512); Bsq16 = Buf()
        rstd = ar.f32(512); Brstd = Buf()
        for qi in range(NT):
            kq = load_q(2, slice(0, 128), slice(0, 128), qi)
            for r in range(2):
                rows = slice(64 * r, 64 * r + 64)
                OB, DB = 3 + 2 * r, 4 + 2 * r
                fw.op("pe", MM(ps[OB][:, :], cbs(cx, C_ZE), KT[:, 0:512], True, False), reads=[cx.Bc, BKT], writes=[pb[OB]], sig=False)
                fw.op("pe", MM(ps[DB][:, :], cbs(cx, C_ZE), KT[:, 0:512], True, False), reads=[cx.Bc, BKT], writes=[pb[DB]], sig=False)
                nb = 4 * qi + 4
                for n in range(nb):
                    kb = n
                    j = kb - 4 * qi
                    c0 = 128 * j if j >= 0 else 0
                    a = 1 + n % 2
                    kw = cnt["w"] % 3
                    cnt["w"] += 1
                    fw.op("pe", MM(ps[a][:, c0:512], KT[rows, kb * 128:(kb + 1) * 128], QT[kq][rows, c0:512], True, True),
                          reads=[BKT, BQT[kq]], writes=[pb[a]])
                    fw.op("act", ACT(w16[kw][:, c0:512], ps[a][:, c0:512], AF.Exp), reads=[pb[a]], writes=[Bw16[kw]])
                    if j >= 0:
                        fw.op("dve", TTo(w16[kw][:, c0:c0 + 128], w16[kw][:, c0:c0 + 128], tri_i, ALU.mult), reads=[Bw16[kw], cx.Bc], writes=[Bw16[kw]])
                    last = (n == nb - 1)
                    fw.op("pe", MM(ps[OB][:, c0:512], V3[:, kb, :], w16[kw][:, c0:512], False, last, skip=True),
                          reads=[BVV, Bw16[kw]], writes=[pb[OB]], sig=False)
                    fw.op("pe", MM(ps[DB][:, c0:512], cbs(cx, C_ON), w16[kw][:, c0:512], False, last, skip=True),
                          reads=[cx.Bc, Bw16[kw]], writes=[pb[DB]], sig=last)
                fw.op("dve", lambda h, r=r, DB=DB: h.reciprocal(out=rd[r], in_=ps[DB][:, :]), reads=[pb[DB]], writes=[Brd[r]])
                fw.op("dve", TTo(tt[r], ps[OB][:, :], rd[r], ALU.mult), reads=[pb[OB], Brd[r]], writes=[Btt[r]])
            fw.op("dve", STT(of, tt[1], sv[:, 2:3], tt[0], ALU.mult, ALU.add), reads=[Btt[0], Btt[1], Bsv], writes=[Bof])
            fw.op("act", ACT(sq16, of, AF.Square), reads=[Bof], writes=[Bsq16])
            fw.op("pe", MM(ps[1][:, :], cbs(cx, C_ON), sq16, True, True), reads=[cx.Bc, Bsq16], writes=[pb[1]])
            fw.op("act", ACT(rstd, ps[1][:, :], AF.Ln, scale=1.0 / 128, bias=cx.epsb[:, 0:1]), reads=[pb[1], cx.Bc], writes=[Brstd])
            fw.op("act", ACT(rstd, rstd, AF.Exp, scale=-0.5), reads=[Brstd], writes=[Brstd])
            k = cnt["o"] % 2
            cnt["o"] += 1
            fw.op("dve", STT(o16[k], of, sv[:, 5:6], rstd, ALU.mult, ALU.mult), reads=[Bof, Bsv, Brstd], writes=[Bo16[k]])
            fw.dma("sp", ot[128:256, qi * 512:(qi + 1) * 512], o16[k], dsO[k], reads=[Bo16[k]])
    else:
        NB = S // 256
        Va = VV[:, 0:NKB * 65].rearrange("p (n c) -> p n c", c=65)
        fw.op("dve", MS(Va[:, :, 64:65], 1.0), writes=[BVV])
        dsI = fw.dsem()
        fw.dma("pool", KT[64:128, :], dr["ind"][:, :], dsI, writes=[BKT])
        km32 = ar.f32(64); Bkm = Buf()
        km16 = ar.bf(64); Bkm16 = Buf()
        Mt = [ar.bf(128) for _ in range(2)]; BMt = [Buf() for _ in range(2)]
        m8 = ar.f32(8); Bm8 = Buf()
        gsb = ar.f32(8); Bgsb = Buf()
        rden = ar.f32(512); Brden = Buf()
        bcs = ar.f32(512); Bbcs = Buf()
        for k in range(2):
            fw.op("dve", MS(Mt[k][:, 0:64], 0.0), writes=[BMt[k]])
        fw.op("dve", MS(km16, 0.0), writes=[Bkm16])
        psT = ps[7][:, :].bitcast(BF16)
        mi = 0
        for hh in range(4):
            gq = 0 if hh < 2 else 2
            rsrc = slice(64 * (hh % 2), 64 * (hh % 2) + 64)
            fw.dma("sp", KT[0:64, :], qk[gq + 1][rsrc, :], dsK, writes=[BKT])
            for n0 in range(0, NKB, 32):
                n1 = min(NKB, n0 + 32)
                fw.dma("sp", Va[:, n0:n1, 0:64], vs3[:, n0:n1, 64 * hh:64 * hh + 64], dsV, writes=[BVV])
            fw.op("dve", lambda h: h.tensor_reduce(out=km32[0:64, 0:NB], in_=KT[0:64, :].rearrange("p (n k) -> p n k", k=256), axis=AX.X, op=ALU.add),
                  reads=[BKT], writes=[Bkm])
            fw.op("dve", TS(km16[0:64, 0:NB], km32[0:64, 0:NB], 1.0 / 256, None, ALU.mult), reads=[Bkm], writes=[Bkm16])
            for qi in range(NT):
                kq = load_q(gq, rsrc, slice(0, 64), qi)
                for i in range(4):
                    own = 2 * qi + i // 2
                    G = ps[6][:, i * 64:(i + 1) * 64]
                    fw.op("pe", MM(G, QT[kq][0:64, i * 128:(i + 1) * 128], km16[0:64, 0:64], True, True), reads=[BQT[kq], Bkm16], writes=[pb[6]])
                    M = Mt[mi % 2]; BM = BMt[mi % 2]
                    mi += 1
                    fw.op("dve", MS(M[:, 64:128], NEG), writes=[BM])
                    if own >= 1:
                        if own <= 3:
                            fw.op("dve", MS(M[:, 64:64 + own], 0.0), writes=[BM])
                        else:
                            if own >= 8:
                                fw.op("dve", lambda h, G=G, own=own: h.max(out=m8, in_=G[:, 0:own]), reads=[pb[6]], writes=[Bm8])
                            else:
                                fw.op("dve", MS(gsb, -1e30), writes=[Bgsb])
                                fw.op("dve", CP(gsb[:, 0:own], G[:, 0:own]), reads=[pb[6]], writes=[Bgsb])
                                fw.op("dve", lambda h: h.max(out=m8, in_=gsb[:, 0:8]), reads=[Bgsb], writes=[Bm8])
                            fw.op("dve", TS(M[:, 64:64 + own], G[:, 0:own], m8[:, 2:3], NEG, ALU.is_lt, ALU.mult), reads=[pb[6], Bm8], writes=[BM])
                    fw.op("dve", MS(M[:, 64 + own:65 + own], 0.0), writes=[BM])
                    fw.op("pe", lambda h, M=M, i=i: h.transpose(out=psT[:, i * 128:(i + 1) * 128], in_=M, identity=cbs(cx, C_ID)),
                          reads=[BM, cx.Bc], writes=[pb[7]])
                fw.op("act", ACT(QT[kq][64:128, :], psT[64:128, 0:512], AF.Copy), reads=[pb[7]], writes=[BQT[kq]])
                OB = 3 + qi % 2
                fw.op("pe", MM(ps[OB][0:65, :], cbs(cx, C_ZE, n=65), KT[:, 0:512], True, False), reads=[cx.Bc, BKT], writes=[pb[OB]], sig=False)
                nb = 4 * qi + 4
                for n in range(nb):
                    kb = n
                    j = kb - 4 * qi
                    c0 = 128 * j if j >= 0 else 0
                    a = 1 + n % 2
                    kw = cnt["w"] % 3
                    cnt["w"] += 1
                    fw.op("pe", MM(ps[a][:, c0:512], KT[:, kb * 128:(kb + 1) * 128], QT[kq][:, c0:512], True, True),
                          reads=[BKT, BQT[kq]], writes=[pb[a]])
                    fw.op("act", ACT(w16[kw][:, c0:512], ps[a][:, c0:512], AF.Exp), reads=[pb[a]], writes=[Bw16[kw]])
                    if j >= 0:
                        fw.op("dve", TTo(w16[kw][:, c0:c0 + 128], w16[kw][:, c0:c0 + 128], tri_i, ALU.mult), reads=[Bw16[kw], cx.Bc], writes=[Bw16[kw]])
                    last = (n == nb - 1)
                    fw.op("pe", MM(ps[OB][0:65, c0:512], Va[:, kb, :], w16[kw][:, c0:512], False, last, skip=True),
                          reads=[BVV, Bw16[kw]], writes=[pb[OB]], sig=last)
                fw.op("dve", lambda h, OB=OB: h.reciprocal(out=rden[64:65, :], in_=ps[OB][64:65, :]), reads=[pb[OB]], writes=[Brden])
                fw.op("pe", MM(ps[5][0:64, :], cx.onesf[64:65, 0:64], rden[64:65, :], True, True), reads=[cx.Bc, Brden], writes=[pb[5]])
                fw.op("act", ACT(bcs[0:64, :], ps[5][0:64, :], AF.Copy), reads=[pb[5]], writes=[Bbcs])
                k = cnt["o"] % 2
                cnt["o"] += 1
                fw.op("dve", TTo(o16[k][0:64, :], ps[OB][0:64, :], bcs[0:64, :], ALU.mult), reads=[pb[OB], Bbcs], writes=[Bo16[k]])
                fw.dma("sp", ot[64 * hh:64 * hh + 64, qi * 512:(qi + 1) * 512], o16[k][0:64, :], dsO[k], reads=[Bo16[k]])
    return [("D", dsO[0], dsO[0].v), ("D", dsO[1], dsO[1].v)]


def small_consts(cx):
    fw = cx.fw
    cx.epsb = fw.sbuf("epsb", [128, 1], F32)
    cx.oneb = fw.sbuf("oneb", [128, 1], F32)
    cx.hpib = fw.sbuf("hpib", [128, 1], F32)
    fw.op("dve", MS(cx.epsb[:], EPS), writes=[cx.Bc])
    fw.op("dve", MS(cx.oneb[:], 1.0), writes=[cx.Bc])
    fw.op("dve", MS(cx.hpib[:], float(np.pi / 2)), writes=[cx.Bc])


TTR = 256


def row_phase(cx, layer, dr, Sown, last):
    fw, nc, ar = cx.fw, cx.nc, cx.ar
    TT = TTR
    NTr = Sown // TT
    ps, pb = cx.ps, cx.pb
    ar.reset()
    wo = ar.bf(8 * 1024); w1 = ar.bf(8 * 4096); wg = ar.bf(8 * 1024); wpj = ar.bf(2 * 1024)
    Bw = Buf("roww")
    dsw = fw.dsem()
    wo3 = wo.rearrange("p (c n) -> p c n", c=8)
    w13 = w1.rearrange("p (c n) -> p c n", c=8)
    wg3 = wg.rearrange("p (c n) -> p c n", c=8)
    wpj3 = wpj.rearrange("p (c n) -> p c n", c=2)
    for c in range(8):
        fw.dma("pool", wo3[:, c, :], dr["wout"][c * 128:(c + 1) * 128, :], dsw, writes=[Bw])
        fw.dma("pool", wg3[:, c, :], dr["wg"][c * 128:(c + 1) * 128, :], dsw, writes=[Bw])
        for hf in range(2):
            fw.dma("pool", w13[:, c, hf * 2048:(hf + 1) * 2048], dr["w1"][c * 128:(c + 1) * 128, hf * 2048:(hf + 1) * 2048], dsw, writes=[Bw])
    for c in range(2):
        fw.dma("pool", wpj3[:, c, :], dr["wpj"][c * 128:(c + 1) * 128, :], dsw, writes=[Bw])
    gm = ar.f32(8); gp = ar.f32(8); gf = ar.f32(8)
    fw.dma("sp", gm, dr["g_mlp"][:, :], dsw, writes=[Bw])
    fw.dma("sp", gp, dr["g_ple"][:, :], dsw, writes=[Bw])
    if last:
        fw.dma("sp", gf, dr["g_fin"][:, :], dsw, writes=[Bw])
    w2blk = [ar.bf(32 * 128) for _ in range(2)]
    Bw2 = [Buf() for _ in range(2)]
    dsw2 = [fw.dsem() for _ in range(2)]
    dsw2o = [fw.dsem() for _ in range(2)]
    w2src = dr["w2"].rearrange("(k p) (f n) -> f p k n", p=128, n=128)
    w2b = dr["w2b"]
    for f in range(8):
        k = f % 2
        blk3 = w2blk[k].rearrange("p (k n) -> p k n", n=128)
        for k0 in range(0, 32, 8):
            fw.dma("pool", blk3[:, k0:k0 + 8, :], w2src[f, :, k0:k0 + 8, :], dsw2[k], writes=[Bw2[k]])
        fw.dma("sp", w2b[f, :, :], w2blk[k], dsw2o[k], reads=[Bw2[k]])
    BW2B = Buf("w2b")
    BW2B.w = None
    w2b_toks = [("D", dsw2o[0], dsw2o[0].v), ("D", dsw2o[1], dsw2o[1].v)]
    hT = [ar.f32(8 * TT) for _ in range(2)]; BhT = [Buf() for _ in range(2)]
    dsh = [fw.dsem() for _ in range(2)]
    mT = [ar.bf(8 * TT) for _ in range(2)]; BmT = [Buf() for _ in range(2)]
    dsm = [fw.dsem() for _ in range(2)]
    pT = [ar.bf(2 * TT) for _ in range(2)]; BpT = [Buf() for _ in range(2)]
    dsp = [fw.dsem() for _ in range(2)]
    uT = ar.bf(8 * TT); BuT = Buf()
    sq = ar.bf(8 * TT); Bsq = Buf()
    rb = ar.f32(TT); Brb = Buf()
    hid = ar.bf(32 * TT); Bhid = Buf()
    r32 = [ar.f32(TT) for _ in range(2)]; Br32 = [Buf() for _ in range(2)]
    sg = [ar.f32(TT) for _ in range(2)]; Bsg = [Buf() for _ in range(2)]
    tm = [ar.f32(TT) for _ in range(2)]; Btm = [Buf() for _ in range(2)]
    dso = [fw.dsem() for _ in range(2)]
    outf = hid.bitcast(F32)
    w2cnt = 0
    for st in range(NTr):
        k2 = st % 2
        h3 = hT[k2].rearrange("p (c t) -> p c t", c=8)
        m3 = mT[k2].rearrange("p (c t) -> p c t", c=8)
        p3 = pT[k2].rearrange("p (c t) -> p c t", c=2)
        u3 = uT.rearrange("p (c t) -> p c t", c=8)
        hd3 = hid.rearrange("p (k t) -> p k t", k=32)
        for c in range(8):
            fw.dma("sp", h3[:, c, :], dr["hin"](c, st), dsh[k2], writes=[BhT[k2]])
            fw.dma("sp", m3[:, c, :], dr["mT"](c, st), dsm[k2], writes=[BmT[k2]])
        for c in range(2):
            fw.dma("pool", p3[:, c, :], dr["pT"][c * 128:(c + 1) * 128, st * TT:(st + 1) * TT], dsp[k2], writes=[BpT[k2]])
        for f in range(8):
            bk = 1 + f % 2
            for c in range(8):
                fw.op("pe", MM(ps[bk][:, 0:TT], wo3[:, c, f * 128:(f + 1) * 128], m3[:, c, :], c == 0, c == 7),
                      reads=[Bw, BmT[k2]], writes=[pb[bk]], sig=(c == 7))
            fw.op("dve", TTo(h3[:, f, :], h3[:, f, :], ps[bk][:, 0:TT], ALU.add), reads=[BhT[k2], pb[bk]], writes=[BhT[k2]])
        rms_feature_major(cx, hT[k2], BhT[k2], gm, sq, Bsq, rb, Brb, uT, BuT, TT, 0)
        for k in range(32):
            bk = 1 + k % 2
            for c in range(8):
                fw.op("pe", MM(ps[bk][:, 0:TT], w13[:, c, k * 128:(k + 1) * 128], u3[:, c, :], c == 0, c == 7),
                      reads=[Bw, BuT], writes=[pb[bk]], sig=(c == 7))
            fw.op("act", ACT(r32[k % 2], ps[bk][:, 0:TT], AF.Relu), reads=[pb[bk]], writes=[Br32[k % 2]])
            fw.op("pool", TTo(hd3[:, k, :], r32[k % 2], r32[k % 2], ALU.mult), reads=[Br32[k % 2]], writes=[Bhid])
        for f in range(8):
            kk = w2cnt % 2
            w2cnt += 1
            e_sp = fw.engs["sp"]
            for t in w2b_toks:
                fw._need(e_sp, t)
            fw.dma("sp", w2blk[kk], w2b[f, :, :], dsw2[kk], writes=[Bw2[kk]])
            blk3 = w2blk[kk].rearrange("p (k n) -> p k n", n=128)
            bk = 3 + f % 2
            for k in range(32):
                fw.op("pe", MM(ps[bk][:, 0:TT], blk3[:, k, :], hd3[:, k, :], k == 0, k == 31),
                      reads=[Bw2[kk], Bhid], writes=[pb[bk]], sig=(k == 31))
            fw.op("dve", TTo(h3[:, f, :], h3[:, f, :], ps[bk][:, 0:TT], ALU.add), reads=[BhT[k2], pb[bk]], writes=[BhT[k2]])
        rms_feature_major(cx, hT[k2], BhT[k2], gp, sq, Bsq, rb, Brb, uT, BuT, TT, 0)
        for f in range(8):
            bg, bp = 5, 6
            for c in range(8):
                fw.op("pe", MM(ps[bg][:, (f % 2) * TT:(f % 2 + 1) * TT], wg3[:, c, f * 128:(f + 1) * 128], u3[:, c, :], c == 0, c == 7),
                      reads=[Bw, BuT], writes=[pb[bg]], sig=(c == 7))
            for c in range(2):
                fw.op("pe", MM(ps[bp][:, (f % 2) * TT:(f % 2 + 1) * TT], wpj3[:, c, f * 128:(f + 1) * 128], p3[:, c, :], c == 0, c == 1),
                      reads=[Bw, BpT[k2]], writes=[pb[bp]], sig=(c == 1))
            j = f % 2
            fw.op("act", ACT(sg[j], ps[bg][:, j * TT:(j + 1) * TT], AF.Sigmoid), reads=[pb[bg]], writes=[Bsg[j]])
            fw.op("dve", TTo(tm[j], sg[j], ps[bp][:, j * TT:(j + 1) * TT], ALU.mult), reads=[Bsg[j], pb[bp]], writes=[Btm[j]])
            fw.op("pool", TTo(h3[:, f, :], h3[:, f, :], tm[j], ALU.add), reads=[BhT[k2], Btm[j]], writes=[BhT[k2]])
        if last:
            rms_feature_major(cx, hT[k2], BhT[k2], gf, sq, Bsq, rb, Brb, None, Bhid, TT, 0, out_f32=outf)
            o3 = outf[:, 0:8 * TT].rearrange("p (c t) -> p c t", c=8)
            for c in range(8):
                fw.dma("sp", dr["hout"](c, st), o3[:, c, :], dso[k2], reads=[Bhid])
        else:
            for c in range(8):
                fw.dma("sp", dr["hout"](c, st), h3[:, c, :], dso[k2], reads=[BhT[k2]])
    return [("D", dso[0], dso[0].v), ("D", dso[1], dso[1].v)]


def build_att(S, layer):
    nc = bass.Bass("TRN2", target_bir_lowering=False)
    fw = FW(nc)
    cx = setup_common(nc, fw, S)
    small_consts(cx)
    dt = lambda name, shape, ty, kind="ExternalInput": nc.dram_tensor(name, list(shape), ty, kind=kind).ap()
    xT = dt("xT", [1024, S], F32)
    dr = {
        "xsrc": lambda c, st: xT[c * 128:(c + 1) * 128, st * 512:(st + 1) * 512],
        "gain": dt("gain", [128, 8], F32),
        "wc": dt("wc", [1024, 768], F32),
        "wp": dt("wp", [1024, 512], F32),
        "pos": dt("pos", [1, S], I32),
        "ot": dt("ot", [256, S], BF16, "ExternalOutput"),
        "qk": [dt(f"qk{i}", [128, S], BF16, "Internal") for i in range(4)],
        "vs": dt("vs", [S, 256], BF16, "Internal"),
    }
    if layer == 0:
        dr["lamv"] = dt("lamv", [4, 64], F32)
        dr["subln"] = dt("subln", [128, 1], F32)
    else:
        dr["ind"] = dt("ind", [64, S], F32)
    toks = att_phase(cx, layer, dr)
    fw.finish(toks)
    return nc


def build_row(S, layer, last):
    Sown = S // 4
    nc = bass.Bass("TRN2", target_bir_lowering=False)
    fw = FW(nc)
    cx = setup_common(nc, fw, S)
    small_consts(cx)
    dt = lambda name, shape, ty, kind="ExternalInput": nc.dram_tensor(name, list(shape), ty, kind=kind).ap()
    hin = dt("hin", [1024, Sown], F32)
    mT = dt("mT", [1024, Sown], BF16)
    hout = dt("hout", [1024, Sown], F32, "ExternalOutput")
    TT = TTR
    dr = {
        "hin": lambda c, st: hin[c * 128:(c + 1) * 128, st * TT:(st + 1) * TT],
        "mT": lambda c, st: mT[c * 128:(c + 1) * 128, st * TT:(st + 1) * TT],
        "hout": lambda c, st: hout[c * 128:(c + 1) * 128, st * TT:(st + 1) * TT],
        "pT": dt("pT", [256, Sown], F32),
        "wout": dt("wout", [1024, 1024], F32),
        "w1": dt("w1", [1024, 4096], F32),
        "w2": dt("w2", [4096, 1024], F32),
        "wg": dt("wg", [1024, 1024], F32),
        "wpj": dt("wpj", [256, 1024], F32),
        "g_mlp": dt("g_mlp", [128, 8], F32),
        "g_ple": dt("g_ple", [128, 8], F32),
        "w2b": dt("w2b", [8, 128, 4096], BF16, "Internal"),
    }
    if last:
        dr["g_fin"] = dt("g_fin", [128, 8], F32)
    toks = row_phase(cx, layer, dr, Sown, last)
    fw.finish(toks)
    return nc


def gcols(g):
    return np.ascontiguousarray(np.asarray(g, np.float32).reshape(8, 128).T)


def partner_perm():
    idx = np.arange(128)
    d = idx % 64
    dp = np.where(d < 8, d + 8, np.where(d < 16, d - 8, d))
    return idx - d + dp


def att_weights(layer, g, ab_w_in, moba_w_in):
    if layer == 0:
        w = ab_w_in[0]
        sl = lambda a, n: w[:, a:a + n]
        G = [sl(128 * g, 128), sl(512 + 128 * g, 128), sl(1536 + 128 * g, 128), sl(2048 + 128 * g, 128)]
        V = [sl(1024 + 128 * g, 128), sl(2560 + 128 * g, 128)]
    else:
        w = moba_w_in[0]
        sl = lambda a, n: w[:, a:a + n]
        G = [sl(256 * g, 128), sl(1024 + 256 * g, 128), sl(256 * g + 128, 128), sl(1024 + 256 * g + 128, 128)]
        V = [sl(2048 + 256 * g, 256)]
    wc = np.ascontiguousarray(np.concatenate(G + V, axis=1), dtype=np.float32)
    pp = partner_perm()
    wp = np.ascontiguousarray(np.concatenate([Gi[:, pp] for Gi in G], axis=1), dtype=np.float32)
    return wc, wp


_PROGS = {}


def _prog(key, builder):
    if key not in _PROGS:
        _PROGS[key] = builder()
    return _PROGS[key]


def run_forward(S, x, p, positions, attn_norm, ab_w_in, ab_w_out, diff_lam_q1, diff_lam_k1,
                diff_lam_q2, diff_lam_k2, diff_subln, moba_w_in, moba_w_out, mlp_norm,
                w_ff1, w_ff2, ple_norm, ple_gate, ple_proj, final_norm, debug=None):
    f32 = lambda a: np.ascontiguousarray(np.asarray(a), dtype=np.float32)
    x = f32(x); p = f32(p)
    positions = np.ascontiguousarray(np.asarray(positions), dtype=np.int32)
    B = x.shape[0]
    Sown = S // 4
    cb, cf = make_consts()
    cores = list(range(8))
    hT = [np.ascontiguousarray(x[b].T) for b in range(B)]
    ind = np.zeros((64, S), np.float32)
    ind[np.arange(S) // 256, np.arange(S)] = 1.0
    outT = None
    for layer in range(2):
        nc = _prog(("att", S, layer), lambda: build_att(S, layer))
        maps = []
        for c in cores:
            b, g = c // 4, c % 4
            wc, wp = att_weights(layer, g, f32(ab_w_in), f32(moba_w_in))
            m = {"cbd": cb, "cfd": cf, "xT": hT[b], "gain": gcols(attn_norm[layer]), "wc": wc, "wp": wp,
                 "pos": positions[b:b + 1, :]}
            if layer == 0:
                m["lamv"] = np.ascontiguousarray(np.stack([f32(diff_lam_q1)[0, g], f32(diff_lam_k1)[0, g],
                                                           f32(diff_lam_q2)[0, g], f32(diff_lam_k2)[0, g]]))
                m["subln"] = f32(diff_subln)[0].reshape(128, 1)
            else:
                m["ind"] = ind
            maps.append(m)
        _t = _time.time()
        res = run_bass_kernel_spmd(nc, maps, core_ids=cores)
        if debug is not None:
            print("att launch", layer, _time.time() - _t, flush=True)
        ots = [np.asarray(r["ot"]) for r in res.results]
        if debug is not None:
            debug[f"ot{layer}"] = ots
        mergedT = []
        for b in range(B):
            if layer == 0:
                rows = [ots[4 * b + g][0:128] for g in range(4)] + [ots[4 * b + g][128:256] for g in range(4)]
            else:
                rows = [ots[4 * b + g] for g in range(4)]
            mergedT.append(np.concatenate(rows, axis=0))
        last = (layer == 1)
        nc = _prog(("row", S, layer), lambda: build_row(S, layer, last))
        maps = []
        wout = f32(ab_w_out)[0] if layer == 0 else f32(moba_w_out)[0]
        for c in cores:
            b, g = c // 4, c % 4
            own = slice(g * Sown, (g + 1) * Sown)
            m = {"cbd": cb, "cfd": cf,
                 "hin": np.ascontiguousarray(hT[b][:, own]),
                 "mT": np.ascontiguousarray(mergedT[b][:, own]),
                 "pT": np.ascontiguousarray(p[layer, b, own, :].T),
                 "wout": wout, "w1": f32(w_ff1)[layer], "w2": f32(w_ff2)[layer],
                 "wg": f32(ple_gate)[layer], "wpj": f32(ple_proj)[layer],
                 "g_mlp": gcols(mlp_norm[layer]), "g_ple": gcols(ple_norm[layer])}
            if last:
                m["g_fin"] = gcols(final_norm)
            maps.append(m)
        _t = _time.time()
        res = run_bass_kernel_spmd(nc, maps, core_ids=cores)
        if debug is not None:
            print("row launch", layer, _time.time() - _t, flush=True)
        houts = [np.asarray(r["hout"]) for r in res.results]
        hT = [np.ascontiguousarray(np.concatenate([houts[4 * b + g] for g in range(4)], axis=1)) for b in range(B)]
        if debug is not None:
            debug[f"h{layer}"] = hT
    out = np.stack([hT[b].T for b in range(B)]).astype(np.float32)
    return np.ascontiguousarray(out)


def kernel(**inputs):
    S = int(np.asarray(inputs["x"]).shape[1])
    return run_forward(S, **inputs)
```

```python
import math
import time as _time
import numpy as np
import ml_dtypes
from contextlib import ExitStack
import concourse.bass as bass
import concourse.mybir as mybir
from concourse.bass_utils import run_bass_kernel_spmd

F32 = mybir.dt.float32
BF16 = mybir.dt.bfloat16
I32 = mybir.dt.int32
AF = mybir.ActivationFunctionType
ALU = mybir.AluOpType
AX = mybir.AxisListType

D = 1024
DFF = 4096
EPS = 1e-6
NEG = -30000.0
TWO_PI_HI = 6.28125
TWO_PI_LO = float(2 * np.pi - 6.28125)


class Buf:
    __slots__ = ("name", "w", "r")

    def __init__(self, name=""):
        self.name = name
        self.w = None
        self.r = {}


class DmaSem:
    def __init__(self, fw, name):
        self.h = fw.es.enter_context(fw.nc.semaphore(name))
        self.v = 0
        self.id = name


class Eng:
    def __init__(self, fw, name, handle, has_sem=True):
        self.name = name
        self.h = handle
        self.sem = fw.es.enter_context(fw.nc.semaphore("s_" + name)) if has_sem else None
        self.nsig = 0
        self.sigs = []
        self.waited = {}
        self.pending = []


class FW:
    def __init__(self, nc):
        self.nc = nc
        self.es = ExitStack()
        self.engs = {}
        for n, h in (("pe", nc.tensor), ("act", nc.scalar), ("dve", nc.vector),
                     ("pool", nc.gpsimd), ("sp", nc.sync)):
            self.engs[n] = Eng(self, n, h, has_sem=(n != "sp"))
        self.nsem = 0
        self.dsems = []

    def dsem(self, name=None):
        if getattr(self, "free", None):
            return self.free.pop()
        self.nsem += 1
        d = DmaSem(self, name or f"d{self.nsem}")
        self.dsems.append(d)
        return d

    def recycle(self):
        self.free = list(self.dsems)

    def sbuf(self, name, shape, dt):
        return self.es.enter_context(self.nc.sbuf_tensor(name, list(shape), dt))

    def psum(self, name, shape, dt):
        return self.es.enter_context(self.nc.psum_tensor(name, list(shape), dt))

    def _need(self, e, tok):
        if tok is None:
            return
        if tok[0] == "E":
            _, e2, idx = tok
            if e2 is e and e.name == "pe":
                return
            v = None
            for j in range(idx, len(e2.sigs)):
                if e2.sigs[j] is not None:
                    v = e2.sigs[j]
                    break
            assert v is not None, f"unsignaled producer {e2.name}#{idx} needed by {e.name}"
            key = "E" + e2.name
            if e.waited.get(key, 0) >= v:
                return
            e.waited[key] = v
            sem = e2.sem
            e.pending.append(lambda h, sem=sem, v=v: h.wait_ge(sem, v))
        else:
            _, ds, v = tok
            key = "D" + ds.id
            if e.waited.get(key, 0) >= v:
                return
            e.waited[key] = v
            e.pending.append(lambda h, ds=ds, v=v: h.wait_ge(ds.h, v))

    def _deps(self, e, reads, writes):
        for b in reads:
            self._need(e, b.w)
        for b in writes:
            self._need(e, b.w)
            for t in b.r.values():
                self._need(e, t)

    def _mark(self, tok, key, reads, writes):
        for b in reads:
            b.r[key] = tok
        for b in writes:
            b.w = tok
            b.r = {}

    def op(self, en, fn, reads=(), writes=(), sig=True):
        e = self.engs[en]
        self._deps(e, reads, writes)
        idx = len(e.sigs)
        if sig:
            e.nsig += 1
            sem = e.sem
            e.pending.append(lambda h, fn=fn, sem=sem: fn(h).then_inc(sem, 1))
            e.sigs.append(e.nsig)
        else:
            e.pending.append(lambda h, fn=fn: fn(h))
            e.sigs.append(None)
        tok = ("E", e, idx)
        self._mark(tok, "E" + e.name, reads, writes)
        return tok

    def dma(self, en, out, in_, ds, reads=(), writes=()):
        e = self.engs[en]
        self._deps(e, reads, writes)
        ds.v += 16

        def emit(h, out=out, in_=in_, ds=ds):
            o = out(h) if callable(out) else out
            i = in_(h) if callable(in_) else in_
            h.dma_start(out=o, in_=i).then_inc(ds.h, 16)
        e.pending.append(emit)
        e.sigs.append(None)
        tok = ("D", ds, ds.v)
        self._mark(tok, "D" + ds.id, reads, writes)
        return tok

    def allgather(self, src, dst, ds):
        e = self.engs["pool"]
        ds.v += 16
        groups = [[0, 1, 2, 3], [4, 5, 6, 7]]
        e.pending.append(lambda h, src=src, dst=dst, ds=ds: h.collective_compute(
            "AllGather", ALU.bypass, replica_groups=groups, ins=[src], outs=[dst]).then_inc(ds.h, 16))
        e.sigs.append(None)
        return ("D", ds, ds.v)

    def barrier(self, scratch):
        toks = []
        pool = self.engs["pool"]
        for ds in self.dsems:
            if ds.v:
                self._need(pool, ("D", ds, ds.v))
        bufs = [Buf() for _ in range(4)]
        toks.append(self.op("pool", lambda h: h.memset(scratch[:, 0:1], 0.0), writes=[bufs[0]]))
        toks.append(self.op("dve", lambda h: h.memset(scratch[:, 1:2], 0.0), writes=[bufs[1]]))
        toks.append(self.op("act", lambda h: h.activation(out=scratch[:, 2:3], in_=scratch[:, 3:4], func=AF.Copy), writes=[bufs[2]]))
        pe = self.engs["pe"]
        if pe.sigs and pe.sigs[-1] is None:
            raise AssertionError("last PE op before barrier must be signaled")
        petok = ("E", pe, len(pe.sigs) - 1) if pe.sigs else None
        for n in ("pe", "act", "dve", "pool", "sp"):
            e = self.engs[n]
            for t in toks:
                self._need(e, t)
            if petok is not None:
                self._need(e, petok)
            for ds in self.dsems:
                if ds.v:
                    self._need(e, ("D", ds, ds.v))

    def finish(self, final_toks, en="sp"):
        for t in final_toks:
            self._need(self.engs[en], t)
        with self.nc.Block() as block:
            for n, dec in (("pe", block.tensor), ("act", block.scalar), ("dve", block.vector),
                           ("pool", block.gpsimd), ("sp", block.sync)):
                e = self.engs[n]
                if not e.pending:
                    continue

                def body(h, e=e):
                    for f in e.pending:
                        f(h)
                dec(body)
        self.es.close()


class Arena:
    def __init__(self, fw, nelem):
        self.t = fw.sbuf("arena", [128, nelem], BF16)
        self.n = nelem
        self.off = 0

    def reset(self, to=0):
        self.off = to

    def bf(self, n):
        assert self.off + n <= self.n, f"arena overflow {self.off}+{n}>{self.n}"
        a = self.t[:, self.off:self.off + n]
        self.off += n
        return a

    def f32(self, n):
        if self.off % 2:
            self.off += 1
        return self.bf(2 * n).bitcast(F32)

    def i32(self, n):
        if self.off % 2:
            self.off += 1
        return self.bf(2 * n).bitcast(I32)


def ACT(out, in_, func, **kw):
    return lambda h: h.activation(out=out, in_=in_, func=func, **kw)


def TTo(out, in0, in1, op):
    return lambda h: h.tensor_tensor(out=out, in0=in0, in1=in1, op=op)


def TS(out, in0, s1, s2, op0, op1=None):
    if op1 is None:
        return lambda h: h.tensor_scalar(out=out, in0=in0, scalar1=s1, scalar2=None, op0=op0)
    return lambda h: h.tensor_scalar(out=out, in0=in0, scalar1=s1, scalar2=s2, op0=op0, op1=op1)


def STT(out, in0, scalar, in1, op0, op1):
    return lambda h: h.scalar_tensor_tensor(out=out, in0=in0, scalar=scalar, in1=in1, op0=op0, op1=op1)


def CP(out, in_):
    return lambda h: h.tensor_copy(out=out, in_=in_)


def MS(ap, c):
    return lambda h: h.memset(ap, c)


def MM(out, lhsT, rhs, start, stop, skip=False):
    if skip:
        return lambda h: h.matmul(out, lhsT=lhsT, rhs=rhs, start=start, stop=stop, skip_group_check=True)
    return lambda h: h.matmul(out, lhsT=lhsT, rhs=rhs, start=start, stop=stop)


C_ID, C_TS, C_TI, C_NTI, C_NO, C_ON, C_ZE = 0, 128, 256, 384, 512, 640, 768
NCB = 896


def make_consts():
    cb = np.zeros((128, NCB), np.float32)
    i = np.arange(128)
    cb[:, C_ID:C_ID + 128] = np.eye(128)
    cb[:, C_TS:C_TS + 128] = (i[:, None] < i[None, :])
    cb[:, C_TI:C_TI + 128] = (i[:, None] <= i[None, :])
    cb[:, C_NTI:C_NTI + 128] = -1.0 * (i[:, None] >= i[None, :])
    cb[:, C_NO:C_NO + 128] = -1.0
    cb[:, C_ON:C_ON + 128] = 1.0
    cf = np.zeros((128, 8), np.float32)
    inv = (np.float32(500000.0) ** (-(np.arange(0, 16, 2, dtype=np.float32) / np.float32(16)))).astype(np.float32)
    for p in range(128):
        d = p % 64
        if d < 16:
            cf[p, 0] = inv[d % 8]
            cf[p, 1] = -1.0 if d < 8 else 1.0
    return cb, cf


class Ctx:
    pass


def setup_common(nc, fw, S):
    cx = Ctx()
    cx.nc, cx.fw, cx.S = nc, fw, S
    cx.ar = Arena(fw, 90 * 1024)
    cx.ps = [fw.psum(f"ps{i}", [128, 512], F32) for i in range(8)]
    cx.pb = [Buf(f"ps{i}") for i in range(8)]
    cx.cb = fw.sbuf("cb", [128, NCB], BF16)
    cx.cf = fw.sbuf("cf", [128, 8], F32)
    cx.scr = fw.sbuf("scr", [128, 8], F32)
    cx.onesf = fw.sbuf("onesf", [128, 64], F32)
    cx.Bc = Buf("consts")
    d_cb = nc.dram_tensor("cbd", [128, NCB], F32, kind="ExternalInput").ap()
    d_cf = nc.dram_tensor("cfd", [128, 8], F32, kind="ExternalInput").ap()
    ds = fw.dsem("dconst")
    fw.dma("pool", cx.cb[:], d_cb[:, :], ds, writes=[cx.Bc])
    fw.dma("sp", cx.cf[:], d_cf[:, :], ds, writes=[cx.Bc])
    fw.op("dve", MS(cx.onesf[:], 1.0), writes=[cx.Bc])
    fw.op("dve", MS(cx.scr[:], 0.0), writes=[cx.Bc])
    return cx


def cbs(cx, off, rows=slice(0, 128), n=128):
    return cx.cb[rows, off:off + n]


def rms_feature_major(cx, xT, Bx, gcol, sq, Bsq, rb, Brb, uT, BuT, T, bank, out_f32=None):
    fw = cx.fw
    for c in range(8):
        fw.op("act", ACT(sq[:, c * T:(c + 1) * T], xT[:, c * T:(c + 1) * T], AF.Square), reads=[Bx], writes=[Bsq])
    for c in range(8):
        fw.op("pe", MM(cx.ps[bank][:, 0:T], cbs(cx, C_ON), sq[:, c * T:(c + 1) * T], c == 0, c == 7),
              reads=[Bsq, cx.Bc], writes=[cx.pb[bank]], sig=(c == 7))
    fw.op("act", ACT(rb[:, 0:T], cx.ps[bank][:, 0:T], AF.Ln, scale=1.0 / D, bias=cx.epsb[:, 0:1]), reads=[cx.pb[bank], cx.Bc], writes=[Brb])
    fw.op("act", ACT(rb[:, 0:T], rb[:, 0:T], AF.Exp, scale=-0.5), reads=[Brb], writes=[Brb])
    for c in range(8):
        o = uT[:, c * T:(c + 1) * T] if out_f32 is None else out_f32[:, c * T:(c + 1) * T]
        fw.op("dve", STT(o, xT[:, c * T:(c + 1) * T], gcol[:, c:c + 1], rb[:, 0:T], ALU.mult, ALU.mult),
              reads=[Bx, Brb, cx.Bc], writes=[BuT])


def att_phase(cx, layer, dr):
    fw, nc, ar, S = cx.fw, cx.nc, cx.ar, cx.S
    NT = S // 512
    NKB = S // 128
    roped = [False, False, True, True] if layer == 0 else [True, True, True, True]
    ar.reset()
    wc = ar.bf(8 * 768)
    wp = ar.bf(8 * 512)
    Bw = Buf("w")
    dsw = fw.dsem()
    wc3 = wc.rearrange("p (c n) -> p c n", c=8)
    wp3 = wp.rearrange("p (c n) -> p c n", c=8)
    for c in range(8):
        fw.dma("pool", wc3[:, c, :], dr["wc"][c * 128:(c + 1) * 128, :], dsw, writes=[Bw])
        fw.dma("pool", wp3[:, c, :], dr["wp"][c * 128:(c + 1) * 128, :], dsw, writes=[Bw])
    for gi in (0, 2):
        fw.op("dve", TS(wc3[:, :, gi * 128:(gi + 1) * 128], wc3[:, :, gi * 128:(gi + 1) * 128], 0.125, None, ALU.mult), reads=[Bw], writes=[Bw])
        fw.op("dve", TS(wp3[:, :, gi * 128:(gi + 1) * 128], wp3[:, :, gi * 128:(gi + 1) * 128], 0.125, None, ALU.mult), reads=[Bw], writes=[Bw])
    gcol = ar.f32(8)
    fw.dma("sp", gcol, dr["gain"][:, :], dsw, writes=[Bw])
    mark_proj = ar.off
    xin = [ar.f32(8 * 512) for _ in range(2)]
    Bxin = [Buf() for _ in range(2)]
    dsx = [fw.dsem() for _ in range(2)]
    sq = ar.bf(8 * 512); Bsq = Buf()
    rb = ar.f32(512); Brb = Buf()
    uT = [ar.bf(8 * 512) for _ in range(2)]
    BuT = [Buf() for _ in range(2)]
    posi = ar.i32(512); Bposi = Buf(); dspos = fw.dsem()
    ang = ar.f32(512); Bang = Buf()
    ki = ar.i32(512); Bki = Buf()
    kf = ar.f32(512); Bkf = Buf()
    Ct = ar.f32(512); BCt = Buf()
    Snt = ar.f32(512); BSn = Buf()
    t1 = [ar.f32(512) for _ in range(2)]; Bt1 = [Buf() for _ in range(2)]
    t2 = [ar.f32(512) for _ in range(2)]; Bt2 = [Buf() for _ in range(2)]
    qko = [[ar.bf(512) for _ in range(2)] for _ in range(4)]
    Bqko = [[Buf() for _ in range(2)] for _ in range(4)]
    dsq = [[fw.dsem() for _ in range(2)] for _ in range(4)]
    vo = [ar.bf(4 * 256) for _ in range(2)]
    Bvo = [Buf() for _ in range(2)]
    dsv = [fw.dsem() for _ in range(2)]
    qk, vs = dr["qk"], dr["vs"]
    vs3 = vs.rearrange("(n p) c -> p n c", p=128)
    pb, ps = cx.pb, cx.ps
    for st in range(NT):
        k2 = st % 2
        x3 = xin[k2].rearrange("p (c t) -> p c t", c=8)
        for c in range(8):
            fw.dma("sp", x3[:, c, :], dr["xsrc"](c, st), dsx[k2], writes=[Bxin[k2]])
        rms_feature_major(cx, xin[k2], Bxin[k2], gcol, sq, Bsq, rb, Brb, uT[k2], BuT[k2], 512, 0)
        u3 = uT[k2].rearrange("p (c t) -> p c t", c=8)
        fw.dma("sp", posi, dr["pos"][0:1, st * 512:(st + 1) * 512].broadcast_to([128, 512]), dspos, writes=[Bposi])
        fw.op("dve", TS(ang, posi, cx.cf[:, 0:1], None, ALU.mult), reads=[Bposi, cx.Bc], writes=[Bang])
        fw.op("dve", TS(ki, ang, float(1.0 / (2 * np.pi)), None, ALU.mult), reads=[Bang], writes=[Bki])
        fw.op("dve", CP(kf, ki), reads=[Bki], writes=[Bkf])
        fw.op("dve", STT(ang, kf, -TWO_PI_HI, ang, ALU.mult, ALU.add), reads=[Bkf, Bang], writes=[Bang])
        fw.op("dve", STT(ang, kf, -TWO_PI_LO, ang, ALU.mult, ALU.add), reads=[Bkf, Bang], writes=[Bang])
        fw.op("dve", TS(ang, ang, float(np.pi), float(-np.pi), ALU.min, ALU.max), reads=[Bang], writes=[Bang])
        fw.op("act", ACT(kf, ang, AF.Sin), reads=[Bang], writes=[Bkf])
        fw.op("dve", TS(Snt, kf, cx.cf[:, 1:2], None, ALU.mult), reads=[Bkf, cx.Bc], writes=[BSn])
        fw.op("dve", STT(ang, ang, -1.0, ang, ALU.mult, ALU.max), reads=[Bang], writes=[Bang])
        fw.op("act", ACT(Ct, ang, AF.Sin, scale=-1.0, bias=cx.hpib[:, 0:1]), reads=[Bang, cx.Bc], writes=[BCt])
        for gi in range(4):
            b1 = 1 + (gi % 2) * 2
            b2 = b1 + 1
            for c in range(8):
                fw.op("pe", MM(ps[b1][:, :], wc3[:, c, gi * 128:(gi + 1) * 128], u3[:, c, :], c == 0, c == 7),
                      reads=[Bw, BuT[k2]], writes=[pb[b1]], sig=(c == 7))
            if roped[gi]:
                for c in range(8):
                    fw.op("pe", MM(ps[b2][:, :], wp3[:, c, gi * 128:(gi + 1) * 128], u3[:, c, :], c == 0, c == 7),
                          reads=[Bw, BuT[k2]], writes=[pb[b2]], sig=(c == 7))
                j2 = gi % 2
                fw.op("dve", TTo(t1[j2], ps[b1][:, :], Ct, ALU.mult), reads=[pb[b1], BCt], writes=[Bt1[j2]])
                fw.op("dve", TTo(t2[j2], ps[b2][:, :], Snt, ALU.mult), reads=[pb[b2], BSn], writes=[Bt2[j2]])
                fw.op("pool", TTo(qko[gi][k2], t1[j2], t2[j2], ALU.add), reads=[Bt1[j2], Bt2[j2]], writes=[Bqko[gi][k2]])
            else:
                fw.op("act", ACT(qko[gi][k2], ps[b1][:, :], AF.Copy), reads=[pb[b1]], writes=[Bqko[gi][k2]])
            fw.dma("sp", qk[gi][:, st * 512:(st + 1) * 512], qko[gi][k2], dsq[gi][k2], reads=[Bqko[gi][k2]])
        vo3 = vo[k2].rearrange("p (t c) -> p t c", t=4)
        for t in range(4):
            bk = 5 + t // 2
            o = ps[bk][:, (t % 2) * 256:(t % 2) * 256 + 256]
            for c in range(8):
                fw.op("pe", MM(o, u3[:, c, t * 128:(t + 1) * 128], wc3[:, c, 512:768], c == 0, c == 7),
                      reads=[Bw, BuT[k2]], writes=[pb[bk]], sig=(c == 7))
            fw.op("act", ACT(vo3[:, t, :], o, AF.Copy), reads=[pb[bk]], writes=[Bvo[k2]])
        fw.dma("sp", vs3[:, st * 4:(st + 1) * 4, :], vo3, dsv[k2], reads=[Bvo[k2]])
    fw.barrier(cx.scr)
    ar.reset(mark_proj)
    KT = ar.bf(S); BKT = Buf("KT")
    VV = ar.bf(NKB * 130); BVV = Buf("VV")
    dsK = fw.dsem(); dsV = fw.dsem()
    QT = [ar.bf(512) for _ in range(2)]; BQT = [Buf() for _ in range(2)]
    dsQ = [fw.dsem() for _ in range(2)]
    e32 = [ar.f32(512) for _ in range(2)]; Be = [Buf() for _ in range(2)]
    sp16 = [ar.bf(512) for _ in range(2)]; Bsp = [Buf() for _ in range(2)]
    w16 = [ar.bf(512) for _ in range(3)]; Bw16 = [Buf() for _ in range(3)]
    L32 = ar.f32(512); BL32 = Buf()
    L16 = [ar.bf(512) for _ in range(3)]; BL16 = [Buf() for _ in range(3)]
    o16 = [ar.bf(512) for _ in range(2)]; Bo16 = [Buf() for _ in range(2)]
    dsO = [fw.dsem() for _ in range(2)]
    ot = dr["ot"]
    tri_s = cbs(cx, C_TS); tri_i = cbs(cx, C_TI)
    cnt = {"q": 0, "o": 0, "w": 0}

    def load_q(gi, rows_src, rows_dst, qi):
        k = cnt["q"] % 2
        cnt["q"] += 1
        fw.dma("sp", QT[k][rows_dst, :], qk[gi][rows_src, qi * 512:(qi + 1) * 512], dsQ[k], writes=[BQT[k]])
        return k

    def store_o(src_ps, bank, nrows, row0, qi, mul=None):
        k = cnt["o"] % 2
        cnt["o"] += 1
        fw.op("act", ACT(o16[k][0:nrows, :], src_ps, AF.Copy), reads=[pb[bank]], writes=[Bo16[k]])
        fw.dma("sp", ot[row0:row0 + nrows, qi * 512:(qi + 1) * 512], o16[k][0:nrows, :], dsO[k], reads=[Bo16[k]])

    if layer == 0:
        V3 = VV[:, 0:NKB * 128].rearrange("p (n c) -> p n c", c=128)
        fw.dma("sp", KT, qk[1][:, :], dsK, writes=[BKT])
        for n0 in range(0, NKB, 32):
            n1 = min(NKB, n0 + 32)
            fw.dma("sp", V3[:, n0:n1, :], vs3[:, n0:n1, 0:128], dsV, writes=[BVV])
        for qi in range(NT):
            kq = load_q(0, slice(0, 128), slice(0, 128), qi)
            for h in range(2):
                rows = slice(64 * h, 64 * h + 64)
                OB = 5 + h
                fw.op("pe", MM(ps[OB][0:64, :], cbs(cx, C_ZE, n=64), KT[:, 0:512], True, False), reads=[cx.Bc, BKT], writes=[pb[OB]], sig=False)
                fw.op("pool", MS(L32, 0.0), writes=[BL32])
                blocks = list(range(4 * qi + 3, -1, -1))
                nb = len(blocks)

                def stage1(n):
                    kb = blocks[n]
                    j = kb - 4 * qi
                    c0 = 128 * j if j >= 0 else 0
                    a = 1 + n % 2
                    k = n % 2
                    fw.op("pe", MM(ps[a][:, c0:512], KT[rows, kb * 128:(kb + 1) * 128], QT[kq][rows, c0:512], True, True),
                          reads=[BKT, BQT[kq]], writes=[pb[a]])
                    fw.op("act", ACT(e32[k][:, c0:512], ps[a][:, c0:512], AF.Exp), reads=[pb[a]], writes=[Be[k]])
                    fw.op("act", ACT(sp16[k][:, c0:512], e32[k][:, c0:512], AF.Ln, bias=cx.oneb[:, 0:1]), reads=[Be[k], cx.Bc], writes=[Bsp[k]])
                    if j >= 0:
                        fw.op("dve", TTo(sp16[k][:, c0:c0 + 128], sp16[k][:, c0:c0 + 128], tri_s, ALU.mult), reads=[Bsp[k], cx.Bc], writes=[Bsp[k]])
                    if n + 1 < nb:
                        fw.op("dve", TTo(L32[:, c0:512], L32[:, c0:512], sp16[k][:, c0:512], ALU.add), reads=[BL32, Bsp[k]], writes=[BL32])
                        fw.op("dve", CP(L16[(n + 1) % 3], L32), reads=[BL32], writes=[BL16[(n + 1) % 3]])

                def stage2(n):
                    kb = blocks[n]
                    j = kb - 4 * qi
                    c0 = 128 * j if j >= 0 else 0
                    b = 3 + n % 2
                    k = n % 2
                    kw = n % 3
                    fw.op("pe", MM(ps[b][:, c0:512], cbs(cx, C_NTI), sp16[k][:, c0:512], True, False), reads=[cx.Bc, Bsp[k]], writes=[pb[b]], sig=False)
                    if n > 0:
                        fw.op("pe", MM(ps[b][:, c0:512], cbs(cx, C_NO), L16[n % 3][:, c0:512], False, False), reads=[cx.Bc, BL16[n % 3]], writes=[pb[b]], sig=False)
                    fw.op("pe", MM(ps[b][:, c0:512], KT[rows, kb * 128:(kb + 1) * 128], QT[kq][rows, c0:512], False, True),
                          reads=[BKT, BQT[kq]], writes=[pb[b]])
                    fw.op("act", ACT(w16[kw][:, c0:512], ps[b][:, c0:512], AF.Exp), reads=[pb[b]], writes=[Bw16[kw]])
                    if j >= 0:
                        fw.op("dve", TTo(w16[kw][:, c0:c0 + 128], w16[kw][:, c0:c0 + 128], tri_s, ALU.mult), reads=[Bw16[kw], cx.Bc], writes=[Bw16[kw]])
                    fw.op("pe", MM(ps[OB][0:64, c0:512], V3[:, kb, 64 * h:64 * h + 64], w16[kw][:, c0:512], False, n == nb - 1, skip=True),
                          reads=[BVV, Bw16[kw]], writes=[pb[OB]], sig=(n == nb - 1))

                for n in range(nb):
                    stage1(n)
                    if n > 0:
                        stage2(n - 1)
                stage2(nb - 1)
                store_o(ps[OB][0:64, :], OB, 64, 64 * h, qi)
        lamt = ar.f32(4 * 64); Blam = Buf()
        lj = ar.f32(64); Blj = Buf()
        sv = ar.f32(8); Bsv = Buf()
        dsl = fw.dsem()
        fw.dma("sp", lamt.rearrange("p (a d) -> p a d", a=4), dr["lamv"].rearrange("(o a) d -> o a d", o=1).broadcast_to([128, 4, 64]), dsl, writes=[Blam])
        fw.dma("sp", sv[:, 4:5], dr["subln"][:, :], dsl, writes=[Bsv])
        fw.op("dve", MS(sv[:, 0:4], 0.0), writes=[Bsv])
        lam_init = 0.8 - 0.6 * math.exp(-0.3 * layer)
        for a in range(2):
            fw.op("dve", TTo(lj, lamt[:, (2 * a) * 64:(2 * a + 1) * 64], lamt[:, (2 * a + 1) * 64:(2 * a + 2) * 64], ALU.mult), reads=[Blam, Bsv], writes=[Blj])
            fw.op("dve", lambda h, a=a: h.tensor_reduce(out=sv[:, a:a + 1], in_=lj, axis=AX.X, op=ALU.add), reads=[Blj], writes=[Bsv])
        fw.op("act", ACT(sv[:, 0:2], sv[:, 0:2], AF.Exp), reads=[Bsv], writes=[Bsv])
        fw.op("dve", TTo(sv[:, 3:4], sv[:, 1:2], sv[:, 0:1], ALU.subtract), reads=[Bsv], writes=[Bsv])
        fw.op("dve", TS(sv[:, 2:3], sv[:, 3:4], -lam_init, None, ALU.add), reads=[Bsv], writes=[Bsv])
        fw.op("dve", TS(sv[:, 5:6], sv[:, 4:5], 1.0 - lam_init, None, ALU.mult), reads=[Bsv], writes=[Bsv])
        fw.dma("sp", KT, qk[3][:, :], dsK, writes=[BKT])
        for n0 in range(0, NKB, 32):
            n1 = min(NKB, n0 + 32)
            fw.dma("sp", V3[:, n0:n1, :], vs3[:, n0:n1, 128:256], dsV, writes=[BVV])
        rd = [ar.f32(512) for _ in range(2)]; Brd = [Buf() for _ in range(2)]
        tt = [ar.f32(512) for _ in range(2)]; Btt = [Buf() for _ in range(2)]
        of = ar.f32(512); Bof = Buf()
        sq16 = ar.bf(512); Bsq16 = Buf()
        rstd = ar.f32(512); Brstd = Buf()
        for qi in range(NT):
            kq = load_q(2, slice(0, 128), slice(0, 128), qi)
            for r in range(2):
                rows = slice(64 * r, 64 * r + 64)
                OB, DB = 3 + 2 * r, 4 + 2 * r
                fw.op("pe", MM(ps[OB][:, :], cbs(cx, C_ZE), KT[:, 0:512], True, False), reads=[cx.Bc, BKT], writes=[pb[OB]], sig=False)
                fw.op("pe", MM(ps[DB][:, :], cbs(cx, C_ZE), KT[:, 0:512], True, False), reads=[cx.Bc, BKT], writes=[pb[DB]], sig=False)
                nb = 4 * qi + 4
                kws = {}

                def stA(n, rows=rows):
                    kb = n
                    j = kb - 4 * qi
                    c0 = 128 * j if j >= 0 else 0
                    a = 1 + n % 2
                    kw = cnt["w"] % 3
                    cnt["w"] += 1
                    kws[n] = kw
                    fw.op("pe", MM(ps[a][:, c0:512], KT[rows, kb * 128:(kb + 1) * 128], QT[kq][rows, c0:512], True, True),
                          reads=[BKT, BQT[kq]], writes=[pb[a]])
                    fw.op("act", ACT(w16[kw][:, c0:512], ps[a][:, c0:512], AF.Exp), reads=[pb[a]], writes=[Bw16[kw]])
                    if j >= 0:
                        fw.op("dve", TTo(w16[kw][:, c0:c0 + 128], w16[kw][:, c0:c0 + 128], tri_i, ALU.mult), reads=[Bw16[kw], cx.Bc], writes=[Bw16[kw]])

                def stB(n, OB=OB, DB=DB):
                    kb = n
                    j = kb - 4 * qi
                    c0 = 128 * j if j >= 0 else 0
                    kw = kws[n]
                    last = (n == nb - 1)
                    fw.op("pe", MM(ps[OB][:, c0:512], V3[:, kb, :], w16[kw][:, c0:512], False, last, skip=True),
                          reads=[BVV, Bw16[kw]], writes=[pb[OB]], sig=False)
                    fw.op("pe", MM(ps[DB][:, c0:512], cbs(cx, C_ON), w16[kw][:, c0:512], False, last, skip=True),
                          reads=[cx.Bc, Bw16[kw]], writes=[pb[DB]], sig=last)

                for n in range(nb):
                    stA(n)
                    if n > 0:
                        stB(n - 1)
                stB(nb - 1)
                fw.op("dve", lambda h, r=r, DB=DB: h.reciprocal(out=rd[r], in_=ps[DB][:, :]), reads=[pb[DB]], writes=[Brd[r]])
                fw.op("dve", TTo(tt[r], ps[OB][:, :], rd[r], ALU.mult), reads=[pb[OB], Brd[r]], writes=[Btt[r]])
            fw.op("dve", STT(of, tt[1], sv[:, 2:3], tt[0], ALU.mult, ALU.add), reads=[Btt[0], Btt[1], Bsv], writes=[Bof])
            fw.op("act", ACT(sq16, of, AF.Square), reads=[Bof], writes=[Bsq16])
            fw.op("pe", MM(ps[1][:, :], cbs(cx, C_ON), sq16, True, True), reads=[cx.Bc, Bsq16], writes=[pb[1]])
            fw.op("act", ACT(rstd, ps[1][:, :], AF.Ln, scale=1.0 / 128, bias=cx.epsb[:, 0:1]), reads=[pb[1], cx.Bc], writes=[Brstd])
            fw.op("act", ACT(rstd, rstd, AF.Exp, scale=-0.5), reads=[Brstd], writes=[Brstd])
            k = cnt["o"] % 2
            cnt["o"] += 1
            fw.op("dve", STT(o16[k], of, sv[:, 5:6], rstd, ALU.mult, ALU.mult), reads=[Bof, Bsv, Brstd], writes=[Bo16[k]])
            fw.dma("sp", ot[128:256, qi * 512:(qi + 1) * 512], o16[k], dsO[k], reads=[Bo16[k]])
    else:
        NB = S // 256
        Va = VV[:, 0:NKB * 65].rearrange("p (n c) -> p n c", c=65)
        fw.op("dve", MS(Va[:, :, 64:65], 1.0), writes=[BVV])
        dsI = fw.dsem()
        fw.dma("pool", KT[64:128, :], dr["ind"][:, :], dsI, writes=[BKT])
        km32 = ar.f32(64); Bkm = Buf()
        km16 = ar.bf(64); Bkm16 = Buf()
        Mt = [ar.bf(128) for _ in range(2)]; BMt = [Buf() for _ in range(2)]
        m8 = ar.f32(8); Bm8 = Buf()
        gsb = ar.f32(8); Bgsb = Buf()
        rden = ar.f32(512); Brden = Buf()
        bcs = ar.f32(512); Bbcs = Buf()
        for k in range(2):
            fw.op("dve", MS(Mt[k][:, 0:64], 0.0), writes=[BMt[k]])
        fw.op("dve", MS(km16, 0.0), writes=[Bkm16])
        psT = ps[7][:, :].bitcast(BF16)
        mi = 0
        for hh in range(4):
            gq = 0 if hh < 2 else 2
            rsrc = slice(64 * (hh % 2), 64 * (hh % 2) + 64)
            fw.dma("sp", KT[0:64, :], qk[gq + 1][rsrc, :], dsK, writes=[BKT])
            for n0 in range(0, NKB, 32):
                n1 = min(NKB, n0 + 32)
                fw.dma("sp", Va[:, n0:n1, 0:64], vs3[:, n0:n1, 64 * hh:64 * hh + 64], dsV, writes=[BVV])
            fw.op("dve", lambda h: h.tensor_reduce(out=km32[0:64, 0:NB], in_=KT[0:64, :].rearrange("p (n k) -> p n k", k=256), axis=AX.X, op=ALU.add),
                  reads=[BKT], writes=[Bkm])
            fw.op("dve", TS(km16[0:64, 0:NB], km32[0:64, 0:NB], 1.0 / 256, None, ALU.mult), reads=[Bkm], writes=[Bkm16])
            for qi in range(NT):
                kq = load_q(gq, rsrc, slice(0, 64), qi)
                for i in range(4):
                    own = 2 * qi + i // 2
                    G = ps[6][:, i * 64:(i + 1) * 64]
                    fw.op("pe", MM(G, QT[kq][0:64, i * 128:(i + 1) * 128], km16[0:64, 0:64], True, True), reads=[BQT[kq], Bkm16], writes=[pb[6]])
                    M = Mt[mi % 2]; BM = BMt[mi % 2]
                    mi += 1
                    fw.op("dve", MS(M[:, 64:128], NEG), writes=[BM])
                    if own >= 1:
                        if own <= 3:
                            fw.op("dve", MS(M[:, 64:64 + own], 0.0), writes=[BM])
                        else:
                            if own >= 8:
                                fw.op("dve", lambda h, G=G, own=own: h.max(out=m8, in_=G[:, 0:own]), reads=[pb[6]], writes=[Bm8])
                            else:
                                fw.op("dve", MS(gsb, -1e30), writes=[Bgsb])
                                fw.op("dve", CP(gsb[:, 0:own], G[:, 0:own]), reads=[pb[6]], writes=[Bgsb])
                                fw.op("dve", lambda h: h.max(out=m8, in_=gsb[:, 0:8]), reads=[Bgsb], writes=[Bm8])
                            fw.op("dve", TS(M[:, 64:64 + own], G[:, 0:own], m8[:, 2:3], NEG, ALU.is_lt, ALU.mult), reads=[pb[6], Bm8], writes=[BM])
                    fw.op("dve", MS(M[:, 64 + own:65 + own], 0.0), writes=[BM])
                    fw.op("pe", lambda h, M=M, i=i: h.transpose(out=psT[:, i * 128:(i + 1) * 128], in_=M, identity=cbs(cx, C_ID)),
                          reads=[BM, cx.Bc], writes=[pb[7]])
                fw.op("act", ACT(QT[kq][64:128, :], psT[64:128, 0:512], AF.Copy), reads=[pb[7]], writes=[BQT[kq]])
                OB = 3 + qi % 2
                fw.op("pe", MM(ps[OB][0:65, :], cbs(cx, C_ZE, n=65), KT[:, 0:512], True, False), reads=[cx.Bc, BKT], writes=[pb[OB]], sig=False)
                nb = 4 * qi + 4
                kws = {}

                def stA(n):
                    kb = n
                    j = kb - 4 * qi
                    c0 = 128 * j if j >= 0 else 0
                    a = 1 + n % 2
                    kw = cnt["w"] % 3
                    cnt["w"] += 1
                    kws[n] = kw
                    fw.op("pe", MM(ps[a][:, c0:512], KT[:, kb * 128:(kb + 1) * 128], QT[kq][:, c0:512], True, True),
                          reads=[BKT, BQT[kq]], writes=[pb[a]])
                    fw.op("act", ACT(w16[kw][:, c0:512], ps[a][:, c0:512], AF.Exp), reads=[pb[a]], writes=[Bw16[kw]])
                    if j >= 0:
                        fw.op("dve", TTo(w16[kw][:, c0:c0 + 128], w16[kw][:, c0:c0 + 128], tri_i, ALU.mult), reads=[Bw16[kw], cx.Bc], writes=[Bw16[kw]])

                def stB(n, OB=OB):
                    kb = n
                    j = kb - 4 * qi
                    c0 = 128 * j if j >= 0 else 0
                    kw = kws[n]
                    last = (n == nb - 1)
                    fw.op("pe", MM(ps[OB][0:65, c0:512], Va[:, kb, :], w16[kw][:, c0:512], False, last, skip=True),
                          reads=[BVV, Bw16[kw]], writes=[pb[OB]], sig=last)

                for n in range(nb):
                    stA(n)
                    if n > 0:
                        stB(n - 1)
                stB(nb - 1)
                fw.op("dve", lambda h, OB=OB: h.reciprocal(out=rden[64:65, :], in_=ps[OB][64:65, :]), reads=[pb[OB]], writes=[Brden])
                fw.op("pe", MM(ps[5][0:64, :], cx.onesf[64:65, 0:64], rden[64:65, :], True, True), reads=[cx.Bc, Brden], writes=[pb[5]])
                fw.op("act", ACT(bcs[0:64, :], ps[5][0:64, :], AF.Copy), reads=[pb[5]], writes=[Bbcs])
                k = cnt["o"] % 2
                cnt["o"] += 1
                fw.op("dve", TTo(o16[k][0:64, :], ps[OB][0:64, :], bcs[0:64, :], ALU.mult), reads=[pb[OB], Bbcs], writes=[Bo16[k]])
                fw.dma("sp", ot[64 * hh:64 * hh + 64, qi * 512:(qi + 1) * 512], o16[k][0:64, :], dsO[k], reads=[Bo16[k]])
    return [("D", dsO[0], dsO[0].v), ("D", dsO[1], dsO[1].v)]


def small_consts(cx):
    fw = cx.fw
    cx.epsb = fw.sbuf("epsb", [128, 1], F32)
    cx.oneb = fw.sbuf("oneb", [128, 1], F32)
    cx.hpib = fw.sbuf("hpib", [128, 1], F32)
    fw.op("dve", MS(cx.epsb[:], EPS), writes=[cx.Bc])
    fw.op("dve", MS(cx.oneb[:], 1.0), writes=[cx.Bc])
    fw.op("dve", MS(cx.hpib[:], float(np.pi / 2)), writes=[cx.Bc])


TTR = 256


def row_phase(cx, layer, dr, Sown, last):
    fw, nc, ar = cx.fw, cx.nc, cx.ar
    TT = TTR
    NTr = Sown // TT
    ps, pb = cx.ps, cx.pb
    ar.reset()
    wo = ar.bf(8 * 1024); w1 = ar.bf(8 * 4096); wg = ar.bf(8 * 1024); wpj = ar.bf(2 * 1024)
    Bw = Buf("roww")
    dsw = fw.dsem()
    wo3 = wo.rearrange("p (c n) -> p c n", c=8)
    w13 = w1.rearrange("p (c n) -> p c n", c=8)
    wg3 = wg.rearrange("p (c n) -> p c n", c=8)
    wpj3 = wpj.rearrange("p (c n) -> p c n", c=2)
    for c in range(8):
        fw.dma("pool", wo3[:, c, :], dr["wout"][c * 128:(c + 1) * 128, :], dsw, writes=[Bw])
        fw.dma("pool", wg3[:, c, :], dr["wg"][c * 128:(c + 1) * 128, :], dsw, writes=[Bw])
        for hf in range(2):
            fw.dma("pool", w13[:, c, hf * 2048:(hf + 1) * 2048], dr["w1"][c * 128:(c + 1) * 128, hf * 2048:(hf + 1) * 2048], dsw, writes=[Bw])
    for c in range(2):
        fw.dma("pool", wpj3[:, c, :], dr["wpj"][c * 128:(c + 1) * 128, :], dsw, writes=[Bw])
    gm = ar.f32(8); gp = ar.f32(8); gf = ar.f32(8)
    fw.dma("sp", gm, dr["g_mlp"][:, :], dsw, writes=[Bw])
    fw.dma("sp", gp, dr["g_ple"][:, :], dsw, writes=[Bw])
    if last:
        fw.dma("sp", gf, dr["g_fin"][:, :], dsw, writes=[Bw])
    w2blk = [ar.bf(32 * 128) for _ in range(2)]
    Bw2 = [Buf() for _ in range(2)]
    dsw2 = [fw.dsem() for _ in range(2)]
    dsw2o = [fw.dsem() for _ in range(2)]
    w2src = dr["w2"].rearrange("(k p) (f n) -> f p k n", p=128, n=128)
    w2b = dr["w2b"]
    for f in range(8):
        k = f % 2
        blk3 = w2blk[k].rearrange("p (k n) -> p k n", n=128)
        for k0 in range(0, 32, 8):
            fw.dma("pool", blk3[:, k0:k0 + 8, :], w2src[f, :, k0:k0 + 8, :], dsw2[k], writes=[Bw2[k]])
        fw.dma("sp", w2b[f, :, :], w2blk[k], dsw2o[k], reads=[Bw2[k]])
    BW2B = Buf("w2b")
    BW2B.w = None
    w2b_toks = [("D", dsw2o[0], dsw2o[0].v), ("D", dsw2o[1], dsw2o[1].v)]
    hT = [ar.f32(8 * TT) for _ in range(2)]; BhT = [Buf() for _ in range(2)]
    dsh = [fw.dsem() for _ in range(2)]
    mT = [ar.bf(8 * TT) for _ in range(2)]; BmT = [Buf() for _ in range(2)]
    dsm = [fw.dsem() for _ in range(2)]
    pT = [ar.bf(2 * TT) for _ in range(2)]; BpT = [Buf() for _ in range(2)]
    dsp = [fw.dsem() for _ in range(2)]
    uT = ar.bf(8 * TT); BuT = Buf()
    sq = ar.bf(8 * TT); Bsq = Buf()
    rb = ar.f32(TT); Brb = Buf()
    hid = ar.bf(32 * TT); Bhid = Buf()
    r32 = [ar.f32(TT) for _ in range(2)]; Br32 = [Buf() for _ in range(2)]
    sg = [ar.f32(TT) for _ in range(2)]; Bsg = [Buf() for _ in range(2)]
    tm = [ar.f32(TT) for _ in range(2)]; Btm = [Buf() for _ in range(2)]
    dso = [fw.dsem() for _ in range(2)]
    outf = hid.bitcast(F32)
    w2cnt = 0
    for st in range(NTr):
        k2 = st % 2
        h3 = hT[k2].rearrange("p (c t) -> p c t", c=8)
        m3 = mT[k2].rearrange("p (c t) -> p c t", c=8)
        p3 = pT[k2].rearrange("p (c t) -> p c t", c=2)
        u3 = uT.rearrange("p (c t) -> p c t", c=8)
        hd3 = hid.rearrange("p (k t) -> p k t", k=32)
        for c in range(8):
            fw.dma("sp", h3[:, c, :], dr["hin"](c, st), dsh[k2], writes=[BhT[k2]])
            fw.dma("sp", m3[:, c, :], dr["mT"](c, st), dsm[k2], writes=[BmT[k2]])
        for c in range(2):
            fw.dma("pool", p3[:, c, :], dr["pT"][c * 128:(c + 1) * 128, st * TT:(st + 1) * TT], dsp[k2], writes=[BpT[k2]])
        for f in range(8):
            bk = 1 + f % 2
            for c in range(8):
                fw.op("pe", MM(ps[bk][:, 0:TT], wo3[:, c, f * 128:(f + 1) * 128], m3[:, c, :], c == 0, c == 7),
                      reads=[Bw, BmT[k2]], writes=[pb[bk]], sig=(c == 7))
            fw.op("dve", TTo(h3[:, f, :], h3[:, f, :], ps[bk][:, 0:TT], ALU.add), reads=[BhT[k2], pb[bk]], writes=[BhT[k2]])
        rms_feature_major(cx, hT[k2], BhT[k2], gm, sq, Bsq, rb, Brb, uT, BuT, TT, 0)
        for k in range(32):
            bk = 1 + k % 2
            for c in range(8):
                fw.op("pe", MM(ps[bk][:, 0:TT], w13[:, c, k * 128:(k + 1) * 128], u3[:, c, :], c == 0, c == 7),
                      reads=[Bw, BuT], writes=[pb[bk]], sig=(c == 7))
            fw.op("act", ACT(r32[k % 2], ps[bk][:, 0:TT], AF.Relu), reads=[pb[bk]], writes=[Br32[k % 2]])
            fw.op("pool", TTo(hd3[:, k, :], r32[k % 2], r32[k % 2], ALU.mult), reads=[Br32[k % 2]], writes=[Bhid])
        for f in range(8):
            kk = w2cnt % 2
            w2cnt += 1
            e_sp = fw.engs["sp"]
            for t in w2b_toks:
                fw._need(e_sp, t)
            fw.dma("sp", w2blk[kk], w2b[f, :, :], dsw2[kk], writes=[Bw2[kk]])
            blk3 = w2blk[kk].rearrange("p (k n) -> p k n", n=128)
            bk = 3 + f % 2
            for k in range(32):
                fw.op("pe", MM(ps[bk][:, 0:TT], blk3[:, k, :], hd3[:, k, :], k == 0, k == 31),
                      reads=[Bw2[kk], Bhid], writes=[pb[bk]], sig=(k == 31))
            fw.op("dve", TTo(h3[:, f, :], h3[:, f, :], ps[bk][:, 0:TT], ALU.add), reads=[BhT[k2], pb[bk]], writes=[BhT[k2]])
        rms_feature_major(cx, hT[k2], BhT[k2], gp, sq, Bsq, rb, Brb, uT, BuT, TT, 0)
        for f in range(8):
            bg, bp = 5, 6
            for c in range(8):
                fw.op("pe", MM(ps[bg][:, (f % 2) * TT:(f % 2 + 1) * TT], wg3[:, c, f * 128:(f + 1) * 128], u3[:, c, :], c == 0, c == 7),
                      reads=[Bw, BuT], writes=[pb[bg]], sig=(c == 7))
            for c in range(2):
                fw.op("pe", MM(ps[bp][:, (f % 2) * TT:(f % 2 + 1) * TT], wpj3[:, c, f * 128:(f + 1) * 128], p3[:, c, :], c == 0, c == 1),
                      reads=[Bw, BpT[k2]], writes=[pb[bp]], sig=(c == 1))
            j = f % 2
            fw.op("act", ACT(sg[j], ps[bg][:, j * TT:(j + 1) * TT], AF.Sigmoid), reads=[pb[bg]], writes=[Bsg[j]])
            fw.op("dve", TTo(tm[j], sg[j], ps[bp][:, j * TT:(j + 1) * TT], ALU.mult), reads=[Bsg[j], pb[bp]], writes=[Btm[j]])
            fw.op("pool", TTo(h3[:, f, :], h3[:, f, :], tm[j], ALU.add), reads=[BhT[k2], Btm[j]], writes=[BhT[k2]])
        if last:
            rms_feature_major(cx, hT[k2], BhT[k2], gf, sq, Bsq, rb, Brb, None, Bhid, TT, 0, out_f32=outf)
            o3 = outf[:, 0:8 * TT].rearrange("p (c t) -> p c t", c=8)
            for c in range(8):
                fw.dma("sp", dr["hout"](c, st), o3[:, c, :], dso[k2], reads=[Bhid])
        else:
            for c in range(8):
                fw.dma("sp", dr["hout"](c, st), h3[:, c, :], dso[k2], reads=[BhT[k2]])
    return [("D", dso[0], dso[0].v), ("D", dso[1], dso[1].v)]


def build_att(S, layer):
    nc = bass.Bass("TRN2", target_bir_lowering=False)
    fw = FW(nc)
    cx = setup_common(nc, fw, S)
    small_consts(cx)
    dt = lambda name, shape, ty, kind="ExternalInput": nc.dram_tensor(name, list(shape), ty, kind=kind).ap()
    xT = dt("xT", [1024, S], F32)
    dr = {
        "xsrc": lambda c, st: xT[c * 128:(c + 1) * 128, st * 512:(st + 1) * 512],
        "gain": dt("gain", [128, 8], F32),
        "wc": dt("wc", [1024, 768], F32),
        "wp": dt("wp", [1024, 512], F32),
        "pos": dt("pos", [1, S], I32),
        "ot": dt("ot", [256, S], BF16, "ExternalOutput"),
        "qk": [dt(f"qk{i}", [128, S], BF16, "Internal") for i in range(4)],
        "vs": dt("vs", [S, 256], BF16, "Internal"),
    }
    if layer == 0:
        dr["lamv"] = dt("lamv", [4, 64], F32)
        dr["subln"] = dt("subln", [128, 1], F32)
    else:
        dr["ind"] = dt("ind", [64, S], F32)
    toks = att_phase(cx, layer, dr)
    fw.finish(toks)
    return nc


def build_row(S, layer, last):
    Sown = S // 4
    nc = bass.Bass("TRN2", target_bir_lowering=False)
    fw = FW(nc)
    cx = setup_common(nc, fw, S)
    small_consts(cx)
    dt = lambda name, shape, ty, kind="ExternalInput": nc.dram_tensor(name, list(shape), ty, kind=kind).ap()
    hin = dt("hin", [1024, Sown], F32)
    mT = dt("mT", [1024, Sown], BF16)
    hout = dt("hout", [1024, Sown], F32, "ExternalOutput")
    TT = TTR
    dr = {
        "hin": lambda c, st: hin[c * 128:(c + 1) * 128, st * TT:(st + 1) * TT],
        "mT": lambda c, st: mT[c * 128:(c + 1) * 128, st * TT:(st + 1) * TT],
        "hout": lambda c, st: hout[c * 128:(c + 1) * 128, st * TT:(st + 1) * TT],
        "pT": dt("pT", [256, Sown], F32),
        "wout": dt("wout", [1024, 1024], F32),
        "w1": dt("w1", [1024, 4096], F32),
        "w2": dt("w2", [4096, 1024], F32),
        "wg": dt("wg", [1024, 1024], F32),
        "wpj": dt("wpj", [256, 1024], F32),
        "g_mlp": dt("g_mlp", [128, 8], F32),
        "g_ple": dt("g_ple", [128, 8], F32),
        "w2b": dt("w2b", [8, 128, 4096], BF16, "Internal"),
    }
    if last:
        dr["g_fin"] = dt("g_fin", [128, 8], F32)
    toks = row_phase(cx, layer, dr, Sown, last)
    fw.finish(toks)
    return nc


def gcols(g):
    return np.ascontiguousarray(np.asarray(g, np.float32).reshape(8, 128).T)


def partner_perm():
    idx = np.arange(128)
    d = idx % 64
    dp = np.where(d < 8, d + 8, np.where(d < 16, d - 8, d))
    return idx - d + dp


def att_weights(layer, g, ab_w_in, moba_w_in):
    if layer == 0:
        w = ab_w_in[0]
        sl = lambda a, n: w[:, a:a + n]
        G = [sl(128 * g, 128), sl(512 + 128 * g, 128), sl(1536 + 128 * g, 128), sl(2048 + 128 * g, 128)]
        V = [sl(1024 + 128 * g, 128), sl(2560 + 128 * g, 128)]
    else:
        w = moba_w_in[0]
        sl = lambda a, n: w[:, a:a + n]
        G = [sl(256 * g, 128), sl(1024 + 256 * g, 128), sl(256 * g + 128, 128), sl(1024 + 256 * g + 128, 128)]
        V = [sl(2048 + 256 * g, 256)]
    wc = np.ascontiguousarray(np.concatenate(G + V, axis=1), dtype=np.float32)
    pp = partner_perm()
    wp = np.ascontiguousarray(np.concatenate([Gi[:, pp] for Gi in G], axis=1), dtype=np.float32)
    return wc, wp


_PROGS = {}


def _prog(key, builder):
    if key not in _PROGS:
        _PROGS[key] = builder()
    return _PROGS[key]


def run_forward(S, x, p, positions, attn_norm, ab_w_in, ab_w_out, diff_lam_q1, diff_lam_k1,
                diff_lam_q2, diff_lam_k2, diff_subln, moba_w_in, moba_w_out, mlp_norm,
                w_ff1, w_ff2, ple_norm, ple_gate, ple_proj, final_norm, debug=None):
    f32 = lambda a: np.ascontiguousarray(np.asarray(a), dtype=np.float32)
    x = f32(x); p = f32(p)
    positions = np.ascontiguousarray(np.asarray(positions), dtype=np.int32)
    B = x.shape[0]
    Sown = S // 4
    cb, cf = make_consts()
    cores = list(range(8))
    hT = [np.ascontiguousarray(x[b].T) for b in range(B)]
    ind = np.zeros((64, S), np.float32)
    ind[np.arange(S) // 256, np.arange(S)] = 1.0
    outT = None
    for layer in range(2):
        nc = _prog(("att", S, layer), lambda: build_att(S, layer))
        maps = []
        for c in cores:
            b, g = c // 4, c % 4
            wc, wp = att_weights(layer, g, f32(ab_w_in), f32(moba_w_in))
            m = {"cbd": cb, "cfd": cf, "xT": hT[b], "gain": gcols(attn_norm[layer]), "wc": wc, "wp": wp,
                 "pos": positions[b:b + 1, :]}
            if layer == 0:
                m["lamv"] = np.ascontiguousarray(np.stack([f32(diff_lam_q1)[0, g], f32(diff_lam_k1)[0, g],
                                                           f32(diff_lam_q2)[0, g], f32(diff_lam_k2)[0, g]]))
                m["subln"] = f32(diff_subln)[0].reshape(128, 1)
            else:
                m["ind"] = ind
            maps.append(m)
        _t = _time.time()
        res = run_bass_kernel_spmd(nc, maps, core_ids=cores)
        if debug is not None:
            print("att launch", layer, _time.time() - _t, flush=True)
        ots = [np.asarray(r["ot"]) for r in res.results]
        if debug is not None:
            debug[f"ot{layer}"] = ots
        mergedT = []
        for b in range(B):
            if layer == 0:
                rows = [ots[4 * b + g][0:128] for g in range(4)] + [ots[4 * b + g][128:256] for g in range(4)]
            else:
                rows = [ots[4 * b + g] for g in range(4)]
            mergedT.append(np.concatenate(rows, axis=0))
        last = (layer == 1)
        nc = _prog(("row", S, layer), lambda: build_row(S, layer, last))
        maps = []
        wout = f32(ab_w_out)[0] if layer == 0 else f32(moba_w_out)[0]
        for c in cores:
            b, g = c // 4, c % 4
            own = slice(g * Sown, (g + 1) * Sown)
            m = {"cbd": cb, "cfd": cf,
                 "hin": np.ascontiguousarray(hT[b][:, own]),
                 "mT": np.ascontiguousarray(mergedT[b][:, own]),
                 "pT": np.ascontiguousarray(p[layer, b, own, :].T),
                 "wout": wout, "w1": f32(w_ff1)[layer], "w2": f32(w_ff2)[layer],
                 "wg": f32(ple_gate)[layer], "wpj": f32(ple_proj)[layer],
                 "g_mlp": gcols(mlp_norm[layer]), "g_ple": gcols(ple_norm[layer])}
            if last:
                m["g_fin"] = gcols(final_norm)
            maps.append(m)
        _t = _time.time()
        res = run_bass_kernel_spmd(nc, maps, core_ids=cores)
        if debug is not None:
            print("row launch", layer, _time.time() - _t, flush=True)
        houts = [np.asarray(r["hout"]) for r in res.results]
        hT = [np.ascontiguousarray(np.concatenate([houts[4 * b + g] for g in range(4)], axis=1)) for b in range(B)]
        if debug is not None:
            debug[f"h{layer}"] = hT
    out = np.stack([hT[b].T for b in range(B)]).astype(np.float32)
    return np.ascontiguousarray(out)


def build_fused(S):
    Sown = S // 4
    TT = TTR
    nc = bass.Bass("TRN2", target_bir_lowering=False)
    fw = FW(nc)
    cx = setup_common(nc, fw, S)
    small_consts(cx)
    dt = lambda name, shape, ty, kind="ExternalInput": nc.dram_tensor(name, list(shape), ty, kind=kind).ap()
    xT = dt("xT", [1024, S], F32)
    xTown = dt("xTown", [1024, Sown], F32)
    pos = dt("pos", [1, S], I32)
    qk = [dt(f"qk{i}", [128, S], BF16, "Internal") for i in range(4)]
    vs = dt("vs", [S, 256], BF16, "Internal")
    ot = dt("ot", [256, S], BF16, "Internal")
    otg = dt("otg", [1024, S], BF16, "Internal")
    h1own = dt("h1own", [1024, Sown], F32, "Internal")
    h1g = dt("h1g", [4096, Sown], F32, "Internal")
    w2b = dt("w2b", [8, 128, 4096], BF16, "Internal")
    hout = dt("hout", [1024, Sown], F32, "ExternalOutput")
    dsag = fw.dsem("dsag")
    gcache = {}

    def core_off(h):
        if "g" not in gcache:
            gcache["g"] = (h.partition_id() % 4) * Sown
        return gcache["g"]

    toks = None
    for layer in range(2):
        L = str(layer)
        if layer == 0:
            xsrc = lambda c, st: xT[c * 128:(c + 1) * 128, st * 512:(st + 1) * 512]
        else:
            def xsrc(c, st):
                r = (st * 512) // Sown
                loc = st * 512 - r * Sown
                return h1g[r * 1024 + c * 128:r * 1024 + (c + 1) * 128, loc:loc + 512]
        dr = {"xsrc": xsrc, "gain": dt("gain" + L, [128, 8], F32), "wc": dt("wc" + L, [1024, 768], F32),
              "wp": dt("wp" + L, [1024, 512], F32), "pos": pos, "ot": ot, "qk": qk, "vs": vs}
        if layer == 0:
            dr["lamv"] = dt("lamv", [4, 64], F32)
            dr["subln"] = dt("subln", [128, 1], F32)
        else:
            dr["ind"] = dt("ind", [64, S], F32)
        att_phase(cx, layer, dr)
        fw.barrier(cx.scr)
        fw.allgather(ot[:, :], otg[:, :], dsag)
        fw.barrier(cx.scr)
        last = (layer == 1)
        hin = xTown if layer == 0 else h1own
        hdst = h1own if layer == 0 else hout

        def mT(c, st, layer=layer):
            if layer == 0:
                r0 = c * 256 if c < 4 else (c - 4) * 256 + 128
            else:
                r0 = (c // 2) * 256 + (c % 2) * 128
            return lambda h, r0=r0, st=st: otg[r0:r0 + 128, bass.ds(core_off(h) + st * TT, TT)]
        dr = {
            "hin": lambda c, st, hin=hin: hin[c * 128:(c + 1) * 128, st * TT:(st + 1) * TT],
            "mT": mT,
            "hout": lambda c, st, hdst=hdst: hdst[c * 128:(c + 1) * 128, st * TT:(st + 1) * TT],
            "pT": dt("pT" + L, [256, Sown], F32),
            "wout": dt("wout" + L, [1024, 1024], F32), "w1": dt("w1" + L, [1024, 4096], F32),
            "w2": dt("w2" + L, [4096, 1024], F32), "wg": dt("wg" + L, [1024, 1024], F32),
            "wpj": dt("wpj" + L, [256, 1024], F32),
            "g_mlp": dt("g_mlp" + L, [128, 8], F32), "g_ple": dt("g_ple" + L, [128, 8], F32), "w2b": w2b,
        }
        if last:
            dr["g_fin"] = dt("g_fin", [128, 8], F32)
        toks = row_phase(cx, layer, dr, Sown, last)
        if not last:
            fw.barrier(cx.scr)
            fw.allgather(h1own[:, :], h1g[:, :], dsag)
            fw.barrier(cx.scr)
    fw.finish(toks)
    return nc


def run_fused(S, x, p, positions, attn_norm, ab_w_in, ab_w_out, diff_lam_q1, diff_lam_k1,
              diff_lam_q2, diff_lam_k2, diff_subln, moba_w_in, moba_w_out, mlp_norm,
              w_ff1, w_ff2, ple_norm, ple_gate, ple_proj, final_norm):
    f32 = lambda a: np.ascontiguousarray(np.asarray(a), dtype=np.float32)
    x = f32(x); p = f32(p)
    positions = np.ascontiguousarray(np.asarray(positions), dtype=np.int32)
    B = x.shape[0]
    Sown = S // 4
    cb, cf = make_consts()
    cores = list(range(8))
    xT = [np.ascontiguousarray(x[b].T) for b in range(B)]
    ind = np.zeros((64, S), np.float32)
    ind[np.arange(S) // 256, np.arange(S)] = 1.0
    nc = _prog(("fused", S), lambda: build_fused(S))
    maps = []
    wouts = [f32(ab_w_out)[0], f32(moba_w_out)[0]]
    for c in cores:
        b, g = c // 4, c % 4
        own = slice(g * Sown, (g + 1) * Sown)
        m = {"cbd": cb, "cfd": cf, "xT": xT[b], "xTown": np.ascontiguousarray(xT[b][:, own]),
             "pos": positions[b:b + 1, :], "ind": ind,
             "lamv": np.ascontiguousarray(np.stack([f32(diff_lam_q1)[0, g], f32(diff_lam_k1)[0, g],
                                                    f32(diff_lam_q2)[0, g], f32(diff_lam_k2)[0, g]])),
             "subln": f32(diff_subln)[0].reshape(128, 1), "g_fin": gcols(final_norm)}
        for layer in range(2):
            L = str(layer)
            wc, wp = att_weights(layer, g, f32(ab_w_in), f32(moba_w_in))
            m.update({"gain" + L: gcols(attn_norm[layer]), "wc" + L: wc, "wp" + L: wp,
                      "pT" + L: np.ascontiguousarray(p[layer, b, own, :].T),
                      "wout" + L: wouts[layer], "w1" + L: f32(w_ff1)[layer], "w2" + L: f32(w_ff2)[layer],
                      "wg" + L: f32(ple_gate)[layer], "wpj" + L: f32(ple_proj)[layer],
                      "g_mlp" + L: gcols(mlp_norm[layer]), "g_ple" + L: gcols(ple_norm[layer])})
        maps.append(m)
    res = run_bass_kernel_spmd(nc, maps, core_ids=cores)
    houts = [np.asarray(r["hout"]) for r in res.results]
    out = np.stack([np.concatenate([houts[4 * b + g] for g in range(4)], axis=1).T for b in range(B)])
    return np.ascontiguousarray(out.astype(np.float32))


def build_solo(S):
    TT = TTR
    nc = bass.Bass("TRN2", target_bir_lowering=False)
    fw = FW(nc)
    cx = setup_common(nc, fw, S)
    small_consts(cx)
    dt = lambda name, shape, ty, kind="ExternalInput": nc.dram_tensor(name, list(shape), ty, kind=kind).ap()
    xT = dt("xT", [1024, S], F32)
    pos = dt("pos", [1, S], I32)
    qk = [dt(f"qk{i}", [128, S], BF16, "Internal") for i in range(4)]
    vs = dt("vs", [S, 256], BF16, "Internal")
    otg = dt("otg", [1024, S], BF16, "Internal")
    h1 = dt("h1", [1024, S], F32, "Internal")
    w2b = dt("w2b", [8, 128, 4096], BF16, "Internal")
    hout = dt("hout", [1024, S], F32, "ExternalOutput")
    lamv = dt("lamv", [16, 64], F32)
    subln = dt("subln", [128, 1], F32)
    ind = dt("ind", [64, S], F32)
    toks = None
    for layer in range(2):
        L = str(layer)
        src = xT if layer == 0 else h1
        gain = dt("gain" + L, [128, 8], F32)
        for g in range(4):
            dr = {"xsrc": lambda c, st, src=src: src[c * 128:(c + 1) * 128, st * 512:(st + 1) * 512],
                  "gain": gain, "wc": dt(f"wc{L}_{g}", [1024, 768], F32), "wp": dt(f"wp{L}_{g}", [1024, 512], F32),
                  "pos": pos, "ot": otg[g * 256:(g + 1) * 256, :], "qk": qk, "vs": vs}
            if layer == 0:
                dr["lamv"] = lamv[4 * g:4 * g + 4, :]
                dr["subln"] = subln
            else:
                dr["ind"] = ind
            att_phase(cx, layer, dr)
            fw.barrier(cx.scr)
            fw.recycle()
        last = (layer == 1)
        hdst = h1 if layer == 0 else hout

        def mT(c, st, layer=layer):
            if layer == 0:
                r0 = c * 256 if c < 4 else (c - 4) * 256 + 128
            else:
                r0 = (c // 2) * 256 + (c % 2) * 128
            return otg[r0:r0 + 128, st * TT:(st + 1) * TT]
        dr = {
            "hin": lambda c, st, src=src: src[c * 128:(c + 1) * 128, st * TT:(st + 1) * TT],
            "mT": mT,
            "hout": lambda c, st, hdst=hdst: hdst[c * 128:(c + 1) * 128, st * TT:(st + 1) * TT],
            "pT": dt("pT" + L, [256, S], F32),
            "wout": dt("wout" + L, [1024, 1024], F32), "w1": dt("w1" + L, [1024, 4096], F32),
            "w2": dt("w2" + L, [4096, 1024], F32), "wg": dt("wg" + L, [1024, 1024], F32),
            "wpj": dt("wpj" + L, [256, 1024], F32),
            "g_mlp": dt("g_mlp" + L, [128, 8], F32), "g_ple": dt("g_ple" + L, [128, 8], F32), "w2b": w2b,
        }
        if last:
            dr["g_fin"] = dt("g_fin", [128, 8], F32)
        toks = row_phase(cx, layer, dr, S, last)
        if not last:
            fw.barrier(cx.scr)
            fw.recycle()
    fw.finish(toks)
    return nc


def run_solo(S, x, p, positions, attn_norm, ab_w_in, ab_w_out, diff_lam_q1, diff_lam_k1,
             diff_lam_q2, diff_lam_k2, diff_subln, moba_w_in, moba_w_out, mlp_norm,
             w_ff1, w_ff2, ple_norm, ple_gate, ple_proj, final_norm):
    f32 = lambda a: np.ascontiguousarray(np.asarray(a), dtype=np.float32)
    x = f32(x); p = f32(p)
    positions = np.ascontiguousarray(np.asarray(positions), dtype=np.int32)
    B = x.shape[0]
    cb, cf = make_consts()
    cores = list(range(8))
    xT = [np.ascontiguousarray(x[b].T) for b in range(B)]
    ind = np.zeros((64, S), np.float32)
    ind[np.arange(S) // 256, np.arange(S)] = 1.0
    nc = _prog(("solo", S), lambda: build_solo(S))
    wouts = [f32(ab_w_out)[0], f32(moba_w_out)[0]]
    lam_all = np.ascontiguousarray(np.concatenate([np.stack([f32(diff_lam_q1)[0, g], f32(diff_lam_k1)[0, g],
                                                             f32(diff_lam_q2)[0, g], f32(diff_lam_k2)[0, g]]) for g in range(4)]))
    base = {"cbd": cb, "cfd": cf, "ind": ind, "lamv": lam_all, "subln": f32(diff_subln)[0].reshape(128, 1),
            "g_fin": gcols(final_norm)}
    for layer in range(2):
        L = str(layer)
        for g in range(4):
            wc, wp = att_weights(layer, g, f32(ab_w_in), f32(moba_w_in))
            base[f"wc{L}_{g}"] = wc
            base[f"wp{L}_{g}"] = wp
        base.update({"gain" + L: gcols(attn_norm[layer]), "wout" + L: wouts[layer], "w1" + L: f32(w_ff1)[layer],
                     "w2" + L: f32(w_ff2)[layer], "wg" + L: f32(ple_gate)[layer], "wpj" + L: f32(ple_proj)[layer],
                     "g_mlp" + L: gcols(mlp_norm[layer]), "g_ple" + L: gcols(ple_norm[layer])})
    maps = []
    for c in cores:
        b = c // 4
        m = dict(base)
        m.update({"xT": xT[b], "pos": positions[b:b + 1, :],
                  "pT0": np.ascontiguousarray(p[0, b].T), "pT1": np.ascontiguousarray(p[1, b].T)})
        maps.append(m)
    res = run_bass_kernel_spmd(nc, maps, core_ids=cores)
    out = np.stack([np.asarray(res.results[4 * b]["hout"]).T for b in range(B)])
    return np.ascontiguousarray(out.astype(np.float32))


def kernel(**inputs):
    S = int(np.asarray(inputs["x"]).shape[1])
    return run_forward(S, **inputs)
```
